# Optimizing a Trainium2 kernel written in Bass

```python
import math
import jax, jax.numpy as jnp
from jax import lax
import numpy as np

D_MODEL = 1024
BATCH = 4
SEQ = 4096
DEPTH = 2

HEAD_DIM = 64
HEADS_PER_GROUP = D_MODEL // (2 * HEAD_DIM)
ATTN_GROUPS = ((128, 1), (512, 4), (2048, 16))
N_GROUPS = len(ATTN_GROUPS)
N_ATTN_HEADS = N_GROUPS * HEADS_PER_GROUP
GROUP_WIDTH = HEADS_PER_GROUP * HEAD_DIM
ATTN_OUT = GROUP_WIDTH
BLOCK = 128
REL_BUCKETS = 32
REL_MAX_DIST = 2048
NEG = -1e30
SSM_WIDTH = D_MODEL // 2
SSM_GROUP = 16
SSM_GROUPS = SSM_WIDTH // SSM_GROUP
SSM_STATE = 64
DT_MIN = 1e-3
DT_MAX = 1e-1
D_FF = 2816
QKV_COLS = 3 * N_GROUPS * GROUP_WIDTH
IN_COLS = QKV_COLS + SSM_WIDTH + 2 * D_MODEL
N_MOD = 9
EPS = 1e-6

kernel_name = "hybrid_dilated_attn_s5_macaron_adaln"


def rmsnorm(x, g):
    xf = x.astype(jnp.float32)
    y = xf * lax.rsqrt(jnp.mean(xf * xf, axis=-1, keepdims=True) + EPS)
    return (y * g.astype(jnp.float32)).astype(x.dtype)


def modulate(h, shift, scale):
    return h * (1 + scale) + shift


def swiglu(h, w_in, w_out):
    a, b = jnp.split(h @ w_in, 2, axis=-1)
    return (jax.nn.silu(a) * b) @ w_out


def t5_bucket(dist):
    max_exact = REL_BUCKETS // 2
    d = np.maximum(dist, max_exact).astype(np.float32)
    large = max_exact + (np.log(d / max_exact) / np.log(REL_MAX_DIST / max_exact)
                         * (REL_BUCKETS - max_exact)).astype(np.int32)
    large = np.minimum(large, REL_BUCKETS - 1)
    return np.where(dist < max_exact, dist, large).astype(np.int32)


def dilated_window_attention(q, k, v, bias_table, window, dilation):
    Bsz, S, H, E = q.shape
    steps = window // dilation
    L = S // dilation
    nb = -(-L // BLOCK)
    Lp = nb * BLOCK

    def to_sub(t):
        return t.reshape(Bsz, L, dilation, H, E).transpose(0, 2, 1, 3, 4)

    qs = jnp.pad(to_sub(q), ((0, 0), (0, 0), (0, Lp - L), (0, 0), (0, 0)))
    qs = qs.reshape(Bsz, dilation, nb, BLOCK, H, E)

    def key_blocks(t):
        t = jnp.pad(to_sub(t), ((0, 0), (0, 0), (BLOCK, Lp - L), (0, 0), (0, 0)))
        t = t.reshape(Bsz, dilation, nb + 1, BLOCK, H, E)
        return jnp.concatenate([t[:, :, :-1], t[:, :, 1:]], axis=3)

    kb, vb = key_blocks(k), key_blocks(v)

    qi = np.arange(BLOCK)[:, None]
    kj = np.arange(2 * BLOCK)[None, :]
    rel = BLOCK + qi - kj
    band = (rel >= 0) & (rel <= steps)
    kpos = (np.arange(nb)[:, None, None] - 1) * BLOCK + kj[None]
    mask = band[None] & (kpos >= 0)
    bucket = t5_bucket(np.clip(rel, 0, None) * dilation)
    bias = jnp.transpose(bias_table[bucket], (2, 0, 1)).astype(jnp.float32)

    logits = jnp.einsum('bdnqhe,bdnkhe->bdnhqk', qs, kb).astype(jnp.float32) * (E ** -0.5) + bias
    logits = jnp.where(mask[:, None], logits, NEG)
    m = jnp.max(logits, axis=-1, keepdims=True)
    p = jnp.exp(logits - m)
    s = jnp.sum(p, axis=-1, keepdims=True)
    out = jnp.einsum('bdnhqk,bdnkhe->bdnqhe', (p / s).astype(v.dtype), vb)
    lse = (m + jnp.log(s))[..., 0]

    out = out.reshape(Bsz, dilation, Lp, H, E)[:, :, :L]
    out = out.transpose(0, 2, 1, 3, 4).reshape(Bsz, S, H, E)
    lse = lse.transpose(0, 1, 2, 4, 3).reshape(Bsz, dilation, Lp, H)[:, :, :L]
    lse = lse.transpose(0, 2, 1, 3).reshape(Bsz, S, H)
    return out, lse


def s5_branch(u, lam_re, lam_im, log_dt, b_re, b_im, c_re, c_im, d_skip, w_glu):
    Bsz, S, _ = u.shape
    u = u.reshape(Bsz, S, SSM_GROUPS, SSM_GROUP)
    dt = jnp.exp(log_dt)[:, None]
    mag = jnp.exp(lam_re * dt)
    ang = lam_im * dt
    a_re = mag * jnp.cos(ang)
    a_im = mag * jnp.sin(ang)
    den = lam_re * lam_re + lam_im * lam_im
    f_re = ((a_re - 1) * lam_re + a_im * lam_im) / den
    f_im = (a_im * lam_re - (a_re - 1) * lam_im) / den
    bb_re = f_re[..., None] * b_re - f_im[..., None] * b_im
    bb_im = f_re[..., None] * b_im + f_im[..., None] * b_re
    bu_re = jnp.einsum('bsgc,gpc->bsgp', u, bb_re)
    bu_im = jnp.einsum('bsgc,gpc->bsgp', u, bb_im)
    shape_a = (1, S, SSM_GROUPS, SSM_STATE)
    ar = jnp.broadcast_to(a_re[None, None], shape_a)
    ai = jnp.broadcast_to(a_im[None, None], shape_a)

    def combine(e1, e2):
        a1r, a1i, b1r, b1i = e1
        a2r, a2i, b2r, b2i = e2
        return (a2r * a1r - a2i * a1i,
                a2r * a1i + a2i * a1r,
                a2r * b1r - a2i * b1i + b2r,
                a2r * b1i + a2i * b1r + b2i)

    _, _, xr, xi = lax.associative_scan(combine, (ar, ai, bu_re, bu_im), axis=1)
    y = (jnp.einsum('bsgp,gcp->bsgc', xr, c_re) - jnp.einsum('bsgp,gcp->bsgc', xi, c_im)
         + d_skip * u)
    y = jax.nn.gelu(y.reshape(Bsz, S, SSM_WIDTH))
    ga, gb = jnp.split(y @ w_glu, 2, axis=-1)
    return ga * jax.nn.sigmoid(gb)


def mixing(h, w_in, rel_bias, lam_re, lam_im, log_dt, b_re, b_im, c_re, c_im, d_skip,
           w_glu, w_attn_proj, w_out):
    Bsz, S, _ = h.shape
    proj = h @ w_in
    qkv, u, gates = jnp.split(proj, [QKV_COLS, QKV_COLS + SSM_WIDTH], axis=-1)
    qkv = qkv.reshape(Bsz, S, 3, N_GROUPS, HEADS_PER_GROUP, HEAD_DIM)
    outs, lses = [], []
    for g, (window, dilation) in enumerate(ATTN_GROUPS):
        tbl = rel_bias[:, g * HEADS_PER_GROUP:(g + 1) * HEADS_PER_GROUP]
        o, l = dilated_window_attention(qkv[:, :, 0, g], qkv[:, :, 1, g], qkv[:, :, 2, g],
                                        tbl, window, dilation)
        outs.append(o)
        lses.append(l)
    w = jax.nn.softmax(jnp.stack(lses), axis=0)[..., None]
    o = jnp.sum(w * jnp.stack(outs).astype(jnp.float32), axis=0).astype(h.dtype)
    y_attn = o.reshape(Bsz, S, ATTN_OUT) @ w_attn_proj
    y_ssm = s5_branch(u, lam_re, lam_im, log_dt, b_re, b_im, c_re, c_im, d_skip, w_glu)
    g_attn, g_ssm = jnp.split(jax.nn.sigmoid(gates), 2, axis=-1)
    return (g_attn * y_attn + g_ssm * y_ssm) @ w_out


def setup_inputs(seed: int = 0) -> dict:
    key = jax.random.key(seed)
    ks = jax.random.split(key, 26)
    f32 = jnp.float32

    def nrm(k, shape, scale):
        return jax.random.normal(k, shape, f32) * scale

    G, P = SSM_GROUPS, SSM_STATE
    return {
        'x': nrm(ks[0], (BATCH, SEQ, D_MODEL), 1.0),
        'c': nrm(ks[1], (BATCH, D_MODEL), 1.0),
        'w_ada': nrm(ks[2], (DEPTH, D_MODEL, N_MOD * D_MODEL), 0.5 * D_MODEL ** -0.5),
        'b_ada': nrm(ks[3], (DEPTH, N_MOD * D_MODEL), 0.02),
        'norm_ffn1': 1.0 + nrm(ks[4], (DEPTH, D_MODEL), 0.05),
        'w_ffn1_in': nrm(ks[5], (DEPTH, D_MODEL, 2 * D_FF), D_MODEL ** -0.5),
        'w_ffn1_out': nrm(ks[6], (DEPTH, D_FF, D_MODEL), D_FF ** -0.5),
        'norm_mix': 1.0 + nrm(ks[7], (DEPTH, D_MODEL), 0.05),
        'w_in': nrm(ks[8], (DEPTH, D_MODEL, IN_COLS), D_MODEL ** -0.5),
        'rel_bias': nrm(ks[9], (REL_BUCKETS, N_ATTN_HEADS), 0.5),
        'lam_re': -0.5 + nrm(ks[10], (DEPTH, G, P), 0.01),
        'lam_im': jnp.tile(math.pi * jnp.arange(P, dtype=f32), (DEPTH, G, 1)),
        'log_dt': jax.random.uniform(ks[11], (DEPTH, G), f32, math.log(DT_MIN), math.log(DT_MAX)),
        'b_re': nrm(ks[12], (DEPTH, G, P, SSM_GROUP), (2 * SSM_GROUP) ** -0.5),
        'b_im': nrm(ks[13], (DEPTH, G, P, SSM_GROUP), (2 * SSM_GROUP) ** -0.5),
        'c_re': nrm(ks[14], (DEPTH, G, SSM_GROUP, P), SSM_STATE ** -0.5),
        'c_im': nrm(ks[15], (DEPTH, G, SSM_GROUP, P), SSM_STATE ** -0.5),
        'd_skip': nrm(ks[16], (DEPTH, G, SSM_GROUP), 1.0),
        'w_glu': nrm(ks[17], (DEPTH, SSM_WIDTH, 2 * D_MODEL), SSM_WIDTH ** -0.5),
        'w_attn_proj': nrm(ks[18], (DEPTH, ATTN_OUT, D_MODEL), ATTN_OUT ** -0.5),
        'w_out': nrm(ks[19], (DEPTH, D_MODEL, D_MODEL), D_MODEL ** -0.5),
        'norm_ffn2': 1.0 + nrm(ks[20], (DEPTH, D_MODEL), 0.05),
        'w_ffn2_in': nrm(ks[21], (DEPTH, D_MODEL, 2 * D_FF), D_MODEL ** -0.5),
        'w_ffn2_out': nrm(ks[22], (DEPTH, D_FF, D_MODEL), D_FF ** -0.5),
        'final_norm': 1.0 + nrm(ks[23], (D_MODEL,), 0.05),
    }


def reference(x, c, w_ada, b_ada, norm_ffn1, w_ffn1_in, w_ffn1_out, norm_mix, w_in,
              rel_bias, lam_re, lam_im, log_dt, b_re, b_im, c_re, c_im, d_skip, w_glu,
              w_attn_proj, w_out, norm_ffn2, w_ffn2_in, w_ffn2_out, final_norm):
    Bsz = x.shape[0]
    c_act = jax.nn.silu(c)
    for l in range(DEPTH):
        mod = (c_act @ w_ada[l] + b_ada[l]).reshape(Bsz, N_MOD, 1, D_MODEL)
        sh1, sc1, g1, sh2, sc2, g2, sh3, sc3, g3 = [mod[:, i] for i in range(N_MOD)]
        h = modulate(rmsnorm(x, norm_ffn1[l]), sh1, sc1)
        x = x + 0.5 * g1 * swiglu(h, w_ffn1_in[l], w_ffn1_out[l])
        h = modulate(rmsnorm(x, norm_mix[l]), sh2, sc2)
        x = x + g2 * mixing(h, w_in[l], rel_bias, lam_re[l], lam_im[l], log_dt[l],
                            b_re[l], b_im[l], c_re[l], c_im[l], d_skip[l], w_glu[l],
                            w_attn_proj[l], w_out[l])
        h = modulate(rmsnorm(x, norm_ffn2[l]), sh3, sc3)
        x = x + 0.5 * g3 * swiglu(h, w_ffn2_in[l], w_ffn2_out[l])
    return rmsnorm(x, final_norm)
```

```python
import math
import os as os_
from contextlib import ExitStack

import numpy as np
import concourse.bass as bass
import concourse.mybir as mybir
from concourse.bass_utils import run_bass_kernel_spmd

F32 = mybir.dt.float32
BF16 = mybir.dt.bfloat16
AF = mybir.ActivationFunctionType
ALU = mybir.AluOpType

D = 1024
KC = 8
TOK = 2048
NTT = 4
TT = 512
DFF = 2816
FC = 22
GROUPS = ((128, 1), (512, 4), (2048, 16))
MAGIC = 12582912.0
TWO_PI = 2.0 * math.pi
NEG = -30000.0
INF = 1 << 40
ARENA = 100352
NKX = 21504
KXC = 2688
GOFFS = (0, 128, 640)


class Sched:
    ENG = ('pe', 'act', 'dve', 'pool', 'sp')

    LIMIT = 2000

    def __init__(self, nc, es, n_dma=6, same_engine_sync=True):
        self.nc = nc
        self.es = es
        self.same = same_engine_sync
        self.ops = {e: [] for e in self.ENG}
        self.sem = {}
        self.cnt = {}
        self.cur = {}
        self.epoch = {}
        for e in self.ENG:
            self._new_sem(e)
        self.dma_keys = {}
        for q in ('sp', 'pool'):
            ks = []
            for i in range(n_dma):
                k = "d_%s%d" % (q, i)
                self._new_sem(k)
                ks.append(k)
            self.dma_keys[q] = ks
        self.dma_rr = {q: 0 for q in self.dma_keys}
        self.waited = {e: {} for e in self.ENG}
        self.res = {}

    def _new_sem(self, stream):
        ep = self.epoch.get(stream, -1) + 1
        self.epoch[stream] = ep
        k = "%s#%d" % (stream, ep)
        self.sem[k] = self.es.enter_context(self.nc.semaphore("s_%s_%d" % (stream, ep)))
        self.cnt[k] = 0
        self.cur[stream] = k
        return k

    def _cover(self, acc):
        name, lo, hi = acc
        segs = self.res.get(name)
        if segs is None:
            segs = [[0, INF, {}, {}]]
        out = []
        new = []
        for s in segs:
            slo, shi = s[0], s[1]
            if shi <= lo or slo >= hi:
                new.append(s)
                continue
            if slo < lo:
                new.append([slo, lo, dict(s[2]), dict(s[3])])
            if shi > hi:
                new.append([hi, shi, dict(s[2]), dict(s[3])])
            s[0] = max(slo, lo)
            s[1] = min(shi, hi)
            new.append(s)
            out.append(s)
        self.res[name] = new
        return out

    def _deps(self, reads, writes):
        deps = {}
        rsegs = [s for a in reads for s in self._cover(a)]
        wsegs = [s for a in writes for s in self._cover(a)]
        for s in rsegs:
            for sk, v in s[2].items():
                if deps.get(sk, 0) < v:
                    deps[sk] = v
        for s in wsegs:
            for dd in (s[2], s[3]):
                for sk, v in dd.items():
                    if deps.get(sk, 0) < v:
                        deps[sk] = v
        return deps, rsegs, wsegs

    @staticmethod
    def _record(rsegs, wsegs, sk, v):
        for s in rsegs:
            if s[3].get(sk, 0) < v:
                s[3][sk] = v
        for s in wsegs:
            s[2] = {sk: v}
            s[3] = {}

    def op(self, eng, fn, reads=(), writes=()):
        deps, rsegs, wsegs = self._deps(reads, writes)
        waits = []
        for sk, v in deps.items():
            if sk.split('#')[0] == eng and (eng == 'pe' or not self.same):
                continue
            if self.waited[eng].get(sk, 0) < v:
                waits.append((sk, v))
                self.waited[eng][sk] = v
        k = self.cur[eng]
        if self.cnt[k] >= self.LIMIT:
            k = self._new_sem(eng)
        self.cnt[k] += 1
        self.ops[eng].append((waits, fn, k, 1))
        self._record(rsegs, wsegs, k, self.cnt[k])

    def dma(self, q, fn, reads=(), writes=()):
        deps, rsegs, wsegs = self._deps(reads, writes)
        ks = self.dma_keys[q]
        stream = ks[self.dma_rr[q] % len(ks)]
        self.dma_rr[q] += 1
        dk = self.cur[stream]
        if self.cnt[dk] > 0 and deps.get(dk, 0) < self.cnt[dk]:
            deps[dk] = self.cnt[dk]
        waits = []
        for sk, v in deps.items():
            if self.waited[q].get(sk, 0) < v:
                waits.append((sk, v))
                self.waited[q][sk] = v
        if self.cnt[dk] >= self.LIMIT:
            dk = self._new_sem(stream)
        self.cnt[dk] += 16
        self.ops[q].append((waits, fn, dk, 16))
        self._record(rsegs, wsegs, dk, self.cnt[dk])

    def coll(self, fn, reads=(), writes=()):
        deps, rsegs, wsegs = self._deps(reads, writes)
        if 'cc' not in self.cur:
            self._new_sem('cc')
        dk = self.cur['cc']
        if self.cnt[dk] > 0 and deps.get(dk, 0) < self.cnt[dk]:
            deps[dk] = self.cnt[dk]
        waits = []
        for sk, v in deps.items():
            if self.waited['pool'].get(sk, 0) < v:
                waits.append((sk, v))
                self.waited['pool'][sk] = v
        self.cnt[dk] += 1
        self.ops['pool'].append((waits, fn, dk, 1))
        self._record(rsegs, wsegs, dk, self.cnt[dk])

    def finish_wait(self, eng, accs):
        deps, _, _ = self._deps(accs, ())
        waits = []
        for sk, v in deps.items():
            if self.waited[eng].get(sk, 0) < v:
                waits.append((sk, v))
                self.waited[eng][sk] = v
        self.ops[eng].append((waits, None, None, 0))

    def emit(self):
        nc = self.nc
        with nc.Block() as block:
            def make(engname):
                def body(e):
                    for (waits, fn, sk, inc) in self.ops[engname]:
                        for (wk, v) in waits:
                            e.wait_ge(self.sem[wk], v)
                        if fn is not None:
                            fn(e).then_inc(self.sem[sk], inc)
                return body
            block.tensor(make('pe'))
            block.scalar(make('act'))
            block.vector(make('dve'))
            block.gpsimd(make('pool'))
            block.sync(make('sp'))


class Tl:
    def __init__(self, ap, res, lo, dims, es):
        self.ap = ap
        self.res = res
        self.lo = lo
        self.dims = list(dims)
        self.es = es
        n = es
        for d_ in dims:
            n *= d_
        self.nb = n

    def a(self, i=None, r=None):
        if i is None:
            return (self.res, self.lo, self.lo + self.nb)
        inner = self.es
        for d_ in self.dims[1:]:
            inner *= d_
        if isinstance(i, tuple):
            return (self.res, self.lo + i[0] * inner, self.lo + i[1] * inner)
        if r is None:
            return (self.res, self.lo + i * inner, self.lo + (i + 1) * inner)
        return (self.res, self.lo + i * inner + r[0] * self.es, self.lo + i * inner + r[1] * self.es)


def build(n_layers=2, pairs=None, dbg=(), stop_after=None):
    if pairs is None:
        pairs = [[0, 1], [2, 3], [4, 5], [6, 7]]
    nc = bass.Bass("TRN2", target_bir_lowering=False)

    def din(name, shape, dt=F32):
        return nc.dram_tensor(name, list(shape), dt, kind="ExternalInput").ap()

    xT_d = din("xT", [D, TOK])
    cT_d = din("cT", [128, 8])
    w_ada_d = din("w_ada_h", [2, D, 4608])
    b_adaT_d = din("b_adaT", [2, 128, 72])
    normsT_d = din("normsT", [128, 56])
    wf_in_d = [din("w_ffn1_in", [2, D, 2 * DFF]), din("w_ffn2_in", [2, D, 2 * DFF])]
    wf_out_d = [din("w_ffn1_out", [2, DFF, D]), din("w_ffn2_out", [2, DFF, D])]
    w_in_d = din("w_in", [2, D, 7168])
    w_glu_d = din("w_glu", [2, 512, 2048])
    w_ap_d = din("w_attn_proj", [2, 512, D])
    w_out_d = din("w_out", [2, D, D])
    biasg_d = din("biasg", [128, 24, 256])
    maskc_d = din("maskc", [128, 256])
    halom_d = din("halom", [128, 128])
    flagb_d = din("flagb", [128, 1])
    spc_d = din("sp_cols", [2, 128, 48])
    bTre_d = din("bT_re", [2, 128, 16, 128])
    bTim_d = din("bT_im", [2, 128, 16, 128])
    cpre_d = din("Cp_re", [2, 128, 16, 128])
    cpim_d = din("Cp_im", [2, 128, 16, 128])
    dmat_d = din("dmat", [2, 128, 4, 128])
    ident_d = din("ident", [128, 128])
    iota_d = din("iota", [128, 512])
    out_d = nc.dram_tensor("outT", [D, TOK], F32, kind="ExternalOutput").ap()
    dbg_d = {}
    for name in dbg:
        dbg_d[name] = nc.dram_tensor("dbg_" + name, [D, TOK], F32, kind="ExternalOutput").ap()

    xk_in = [[nc.dram_tensor("xk_in%d_%d" % (l, c), [128, 2 * KXC], BF16) for c in range(4)] for l in range(2)]
    xk_out = [[nc.dram_tensor("xk_out%d_%d" % (l, c), [256, 2 * KXC], BF16) for c in range(4)] for l in range(2)]
    sx_in = [nc.dram_tensor("sx_in%d" % l, [128, 32], F32) for l in range(2)]
    sx_out = [nc.dram_tensor("sx_out%d" % l, [256, 32], F32) for l in range(2)]
    md_in = nc.dram_tensor("md_in", [128, 72], F32)
    md_out = nc.dram_tensor("md_out", [256, 72], F32)

    with ExitStack() as es:
        S = Sched(nc, es)

        def sb(name, dims, dt):
            t = es.enter_context(nc.sbuf_tensor("sb_" + name, [128] + list(dims), dt))
            esz = 4 if dt == F32 else 2
            ap = t[:, :] if len(dims) == 1 else (t[:, :, :] if len(dims) == 2 else t[:, :, :, :])
            return Tl(ap, name, 0, dims, esz)

        XT = sb("XT", [KC, TOK], F32)
        HT = sb("HT", [KC, TOK], BF16)
        arena_t = es.enter_context(nc.sbuf_tensor("arena", [128, ARENA // 2], BF16))

        def ar(off, dims, dt):
            esz = 4 if dt == F32 else 2
            n = esz
            for d_ in dims:
                n *= d_
            assert off % 4 == 0 and off + n <= ARENA, (off, n)
            ap = arena_t[:, off // 2:(off + n) // 2]
            if dt == F32:
                ap = ap.bitcast(F32)
            if len(dims) == 2:
                ap = ap.rearrange("p (a b) -> p a b", a=dims[0])
            elif len(dims) == 3:
                ap = ap.rearrange("p (a b c) -> p a b c", a=dims[0], b=dims[1])
            return Tl(ap, "arena", off, dims, esz)

        PS = []
        for i in range(6):
            t = es.enter_context(nc.psum_tensor("ps%d" % i, [128, 512], F32))
            PS.append(Tl(t[:, :], "ps%d" % i, 0, [512], 4))
        PSB = []
        for i in range(2):
            t = es.enter_context(nc.psum_tensor("psb%d" % i, [128, 1024], BF16))
            PSB.append(Tl(t[:, :], "psb%d" % i, 0, [1024], 2))

        def ps_bf(i):
            return PS[i].ap.bitcast(BF16)

        ident_f = sb("ident_f", [128], F32)
        ident_b = sb("ident_b", [128], BF16)
        ones_b = sb("ones_b", [128], BF16)
        ones_f = sb("ones_f", [128], F32)
        iota_f = sb("iota_f", [512], F32)
        cT = sb("cT", [8], F32)
        cact = sb("cact", [8], BF16)
        modT = sb("modT", [2, 72], F32)
        badaT = sb("badaT", [2, 72], F32)
        normsT = sb("normsT", [56], F32)
        colA = sb("colA", [2, 3, 8], F32)
        colG = sb("colG", [2, 2, 8], F32)
        cst = sb("cst", [4], F32)
        maskc = sb("maskc", [256], F32)
        halom = sb("halom", [128], F32)
        scol = sb("scol", [24, 16], F32)
        ycol = sb("ycol", [32], F32)
        ytmp = sb("ytmp", [6, 32], F32)
        carr = sb("carr", [32], F32)
        cimp = sb("cimp", [32], F32)
        ctmp = sb("ctmp", [4], F32)
        psums = sb("psums", [4, 4, 16], F32)
        carr4 = sb("carr4", [4, 32], F32)
        ccol = sb("ccol", [10, 16], F32)

        def MM(o, oa, l, la, r, ra, st, sp_):
            S.op('pe', lambda e: e.matmul(o, lhsT=l, rhs=r, start=st, stop=sp_), reads=[la, ra], writes=[oa])

        def TR(o, oa, i, ia):
            S.op('pe', lambda e: e.transpose(o, i, ident_b.ap), reads=[ia, ident_b.a()], writes=[oa])

        def ACT(o, oa, i, ia, func, bias=None, scale=None, reads=()):
            kw = {}
            if bias is not None:
                kw['bias'] = bias
            if scale is not None:
                kw['scale'] = scale
            S.op('act', lambda e: e.activation(out=o, in_=i, func=func, **kw), reads=[ia] + list(reads), writes=[oa])

        def TT_(eng, o, oa, a, aa, b, ba, op):
            S.op(eng, lambda e: e.tensor_tensor(out=o, in0=a, in1=b, op=op), reads=[aa, ba], writes=[oa])

        def TS(eng, o, oa, a, aa, s1, s2, op0, op1=None, reads=()):
            if op1 is None:
                S.op(eng, lambda e: e.tensor_scalar(out=o, in0=a, scalar1=s1, scalar2=None, op0=op0),
                     reads=[aa] + list(reads), writes=[oa])
            else:
                S.op(eng, lambda e: e.tensor_scalar(out=o, in0=a, scalar1=s1, scalar2=s2, op0=op0, op1=op1),
                     reads=[aa] + list(reads), writes=[oa])

        def STT(o, oa, a, aa, sc, b, ba, op0, op1, reads=()):
            S.op('dve', lambda e: e.scalar_tensor_tensor(out=o, in0=a, scalar=sc, in1=b, op0=op0, op1=op1),
                 reads=[aa, ba] + list(reads), writes=[oa])

        def CP(eng, o, oa, i, ia):
            if eng == 'act':
                S.op('act', lambda e: e.copy(out=o, in_=i), reads=[ia], writes=[oa])
            else:
                S.op(eng, lambda e: e.tensor_copy(out=o, in_=i), reads=[ia], writes=[oa])

        def DMA(q, o, oa, i, ia):
            S.dma(q, lambda e: e.dma_start(out=o, in_=i), reads=[ia], writes=[oa])

        def dram(name):
            return (name, 0, INF)

        psrr = [0]

        def psum():
            i = psrr[0] % 6
            psrr[0] += 1
            return PS[i]

        DMA('sp', ident_f.ap, ident_f.a(), ident_d, dram("ident"))
        DMA('pool', ident_b.ap, ident_b.a(), ident_d, dram("ident"))
        DMA('sp', iota_f.ap, iota_f.a(), iota_d, dram("iota"))
        DMA('sp', cT.ap, cT.a(), cT_d, dram("cT"))
        DMA('sp', normsT.ap, normsT.a(), normsT_d, dram("normsT"))
        DMA('sp', badaT.ap, badaT.a(), b_adaT_d.rearrange("l p c -> p l c"), dram("b_adaT"))
        DMA('sp', maskc.ap, maskc.a(), maskc_d, dram("maskc"))
        DMA('sp', halom.ap, halom.a(), halom_d, dram("halom"))
        DMA('sp', cst.ap[:, 2:3], cst.a(), flagb_d, dram("flagb"))
        for m in range(KC):
            DMA('sp', XT.ap[:, m, :], XT.a(m), xT_d[m * 128:(m + 1) * 128, :], dram("xT"))
        S.op('pool', lambda e: e.memset(ones_b.ap, 1.0), writes=[ones_b.a()])
        S.op('pool', lambda e: e.memset(ones_f.ap, 1.0), writes=[ones_f.a()])
        S.op('pool', lambda e: e.memset(cst.ap[:, 0:1], 1e-6), writes=[cst.a()])
        S.op('pool', lambda e: e.memset(cst.ap[:, 1:2], math.pi / 2), writes=[cst.a()])
        S.op('pool', lambda e: e.memset(cst.ap[:, 3:4], 0.0), writes=[cst.a()])
        ACT(cact.ap, cact.a(), cT.ap, cT.a(), AF.Silu)

        wad = [ar(0, [KC, 512], BF16), ar(8192, [KC, 512], BF16)]
        madx = sb("madx", [72], F32)
        mall = sb("mall", [2, 72], F32)
        pm = psum()
        npc = 0
        for l in range(2):
            wv_ = w_ada_d[l].rearrange("(kc p) n -> p kc n", p=128)
            for pc in range(9):
                w = wad[npc % 2]
                npc += 1
                DMA('pool', w.ap, w.a(), wv_[:, :, pc * 512:(pc + 1) * 512], dram("w_ada_h"))
                for cc in range(4):
                    col = l * 36 + pc * 4 + cc
                    for k in range(KC):
                        MM(pm.ap[:, col:col + 1], pm.a(), w.ap[:, k, cc * 128:(cc + 1) * 128], w.a(),
                           cact.ap[:, k:k + 1], cact.a(), k == 0, k == KC - 1)
        CP('dve', madx.ap, madx.a(), pm.ap[:, 0:72], pm.a())
        DMA('sp', md_in.ap(), dram("md_in"), madx.ap, madx.a())
        S.coll(lambda e: e.collective_compute("AllGather", ALU.bypass, replica_groups=pairs,
                                              ins=[md_in.ap().opt()], outs=[md_out.ap().opt()]),
               reads=[dram("md_in")], writes=[dram("md_out")])
        DMA('sp', mall.ap[:, 0, :], mall.a(0), md_out.ap()[0:128, :], dram("md_out"))
        DMA('sp', mall.ap[:, 1, :], mall.a(1), md_out.ap()[128:256, :], dram("md_out"))
        for l in range(n_layers):
            for hf in range(2):
                TT_('dve', modT.ap[:, l, hf * 36:(hf + 1) * 36], modT.a(l), mall.ap[:, hf, l * 36:(l + 1) * 36], mall.a(hf),
                    badaT.ap[:, l, hf * 36:(hf + 1) * 36], badaT.a(l), ALU.add)
            for wi, sci in enumerate((1, 4, 7)):
                TS('dve', colA.ap[:, l, wi, :], colA.a(l), modT.ap[:, l, sci * 8:sci * 8 + 8], modT.a(l), 1.0, None, ALU.add)
                TT_('dve', colA.ap[:, l, wi, :], colA.a(l), colA.ap[:, l, wi, :], colA.a(l),
                    normsT.ap[:, l * 24 + wi * 8:l * 24 + wi * 8 + 8], normsT.a(), ALU.mult)
            for wi, gi in enumerate((2, 8)):
                TS('dve', colG.ap[:, l, wi, :], colG.a(l), modT.ap[:, l, gi * 8:gi * 8 + 8], modT.a(l), 0.5, None, ALU.mult)

        NT0 = 90112
        sq = [ar(NT0, [512], BF16), ar(NT0 + 1024, [512], BF16)]
        rs = ar(NT0 + 2048, [512], F32)
        ntm = [ar(NT0 + 4096, [512], F32), ar(NT0 + 6144, [512], F32)]

        def norm(acol, bcol, dst_bf):
            for tt in range(NTT):
                tsl = slice(tt * TT, (tt + 1) * TT)
                pss = psum()
                for m in range(KC):
                    s_ = sq[m % 2]
                    ACT(s_.ap, s_.a(), XT.ap[:, m, tsl], XT.a(m, (tt * TT, (tt + 1) * TT)), AF.Square)
                    MM(pss.ap, pss.a(), ones_b.ap, ones_b.a(), s_.ap, s_.a(), m == 0, m == KC - 1)
                ACT(rs.ap, rs.a(), pss.ap, pss.a(), AF.Sqrt, bias=cst.ap[:, 0:1], scale=1.0 / D, reads=[cst.a()])
                S.op('dve', lambda e: e.reciprocal(out=rs.ap, in_=rs.ap), reads=[rs.a()], writes=[rs.a()])
                for m in range(KC):
                    xa = XT.a(m, (tt * TT, (tt + 1) * TT))
                    a_ap, a_acc = acol(m)
                    if dst_bf:
                        t_ = ntm[m % 2]
                        STT(t_.ap, t_.a(), XT.ap[:, m, tsl], xa, a_ap, rs.ap, rs.a(), ALU.mult, ALU.mult, reads=[a_acc])
                        b_ap, b_acc = bcol(m)
                        ACT(HT.ap[:, m, tsl], HT.a(m, (tt * TT, (tt + 1) * TT)), t_.ap, t_.a(), AF.Identity,
                            bias=b_ap, scale=1.0, reads=[b_acc])
                    else:
                        STT(XT.ap[:, m, tsl], xa, XT.ap[:, m, tsl], xa, a_ap, rs.ap, rs.a(), ALU.mult, ALU.mult,
                            reads=[a_acc])

        def norm_mod(l, wi):
            shi = (0, 3, 6)[wi]
            norm(lambda m: (colA.ap[:, l, wi, m:m + 1], colA.a(l)),
                 lambda m: (modT.ap[:, l, shi * 8 + m:shi * 8 + m + 1], modT.a(l)), True)

        def ffn(l, which):
            win = wf_in_d[which][l].rearrange("(kc p) n -> p kc n", p=128)
            wout = wf_out_d[which][l].rearrange("(j p) n -> p j n", p=128)
            JH = FC // 2
            sT = ar(0, [JH, TOK], BF16)
            wa = [ar(45056 + i * 2048, [KC, 128], BF16) for i in range(2)]
            wb = [ar(49152 + i * 2048, [KC, 128], BF16) for i in range(2)]
            wo = [ar(53248 + i * 2816, [JH, 128], BF16) for i in range(2)]
            sg = [ar(58880 + i * 2048, [512], F32) for i in range(2)]
            wname = "w_ffn%d_in" % (which + 1)
            woname = "w_ffn%d_out" % (which + 1)
            for jh in range(2):
                def ld1(jj):
                    j = jh * JH + jj
                    DMA('pool', wa[jj % 2].ap, wa[jj % 2].a(), win[:, :, j * 128:(j + 1) * 128], dram(wname))
                    DMA('pool', wb[jj % 2].ap, wb[jj % 2].a(), win[:, :, DFF + j * 128:DFF + (j + 1) * 128], dram(wname))

                def ld2(m):
                    DMA('pool', wo[m % 2].ap, wo[m % 2].a(), wout[:, jh * JH:(jh + 1) * JH, m * 128:(m + 1) * 128], dram(woname))
                ld1(0)
                for jj in range(JH):
                    if jj + 1 < JH:
                        ld1(jj + 1)
                    else:
                        ld2(0)
                    for tt in range(NTT):
                        tsl = slice(tt * TT, (tt + 1) * TT)
                        pa = psum()
                        pb = psum()
                        for k in range(KC):
                            MM(pa.ap, pa.a(), wa[jj % 2].ap[:, k, :], wa[jj % 2].a(), HT.ap[:, k, tsl],
                               HT.a(k, (tt * TT, (tt + 1) * TT)), k == 0, k == KC - 1)
                        for k in range(KC):
                            MM(pb.ap, pb.a(), wb[jj % 2].ap[:, k, :], wb[jj % 2].a(), HT.ap[:, k, tsl],
                               HT.a(k, (tt * TT, (tt + 1) * TT)), k == 0, k == KC - 1)
                        g_ = sg[tt % 2]
                        ACT(g_.ap, g_.a(), pa.ap, pa.a(), AF.Silu)
                        TT_('dve', sT.ap[:, jj, tsl], sT.a(jj, (tt * TT, (tt + 1) * TT)), g_.ap, g_.a(), pb.ap, pb.a(), ALU.mult)
                for m in range(KC):
                    if m + 1 < KC:
                        ld2(m + 1)
                    for tt in range(NTT):
                        tsl = slice(tt * TT, (tt + 1) * TT)
                        po = psum()
                        for jj in range(JH):
                            MM(po.ap, po.a(), wo[m % 2].ap[:, jj, :], wo[m % 2].a(), sT.ap[:, jj, tsl],
                               sT.a(jj, (tt * TT, (tt + 1) * TT)), jj == 0, jj == JH - 1)
                        xa = XT.a(m, (tt * TT, (tt + 1) * TT))
                        STT(XT.ap[:, m, tsl], xa, po.ap, po.a(), colG.ap[:, l, which, m:m + 1], XT.ap[:, m, tsl], xa,
                            ALU.mult, ALU.add, reads=[colG.a(l)])

        def winv(l):
            return w_in_d[l].rearrange("(kc p) n -> p kc n", p=128)

        def blk_tokens(g, blk):
            d = GROUPS[g][1]
            nb = (TOK // d) // 128
            r, n = divmod(blk, nb)
            start = 128 * n * d + r
            return r, n, slice(start, start + 127 * d + 1, d) if d > 1 else slice(start, start + 128)

        def halo_export(l):
            R0 = 32768
            kexp = ar(R0, [4, KXC], BF16)
            vexp = ar(R0 + 21504, [4, KXC], BF16)
            wk = [ar(R0 + 43008 + i * 2048, [KC, 128], BF16) for i in range(2)]
            wv = [ar(R0 + 47104 + i * 2048, [KC, 128], BF16) for i in range(2)]
            wi = winv(l)
            it = 0
            for c in range(4):
                for g in range(3):
                    d = GROUPS[g][1]
                    kcol = 1536 + g * 512 + c * 128
                    vcol = 3072 + g * 512 + c * 128
                    wk_, wv_ = wk[it % 2], wv[it % 2]
                    it += 1
                    DMA('pool', wk_.ap, wk_.a(), wi[:, :, kcol:kcol + 128], dram("w_in"))
                    DMA('pool', wv_.ap, wv_.a(), wi[:, :, vcol:vcol + 128], dram("w_in"))
                    o0 = GOFFS[g]
                    if g == 0:
                        p = psum()
                        for k in range(KC):
                            MM(p.ap[:, 0:128], p.a(), wk_.ap[:, k, :], wk_.a(), HT.ap[:, k, 1920:2048], HT.a(k), k == 0, k == KC - 1)
                        CP('act', kexp.ap[:, c, o0:o0 + 128], kexp.a(c), p.ap[:, 0:128], p.a())
                    elif g == 1:
                        p = psum()
                        for k in range(KC):
                            MM(p.ap, p.a(), wk_.ap[:, k, :], wk_.a(), HT.ap[:, k, 1536:2048], HT.a(k), k == 0, k == KC - 1)
                        CP('act', kexp.ap[:, c, o0:o0 + 512].rearrange("p (r j) -> p r j", r=4), kexp.a(c),
                           p.ap.rearrange("p (j r) -> p r j", r=4), p.a())
                    else:
                        for tt in range(NTT):
                            p = psum()
                            for k in range(KC):
                                MM(p.ap, p.a(), wk_.ap[:, k, :], wk_.a(), HT.ap[:, k, tt * TT:(tt + 1) * TT], HT.a(k),
                                   k == 0, k == KC - 1)
                            CP('act' if tt % 2 == 0 else 'dve',
                               kexp.ap[:, c, o0:o0 + 2048].rearrange("p (r j) -> p r j", r=16)[:, :, 32 * tt:32 * tt + 32],
                               kexp.a(c), p.ap.rearrange("p (j r) -> p r j", r=16), p.a())
                    nb = (TOK // d) // 128
                    for r in range(d):
                        blk = r * nb + (nb - 1)
                        _, _, cols = blk_tokens(g, blk)
                        p = psum()
                        for k in range(KC):
                            MM(p.ap[:, 0:128], p.a(), HT.ap[:, k, cols], HT.a(k), wv_.ap[:, k, :], wv_.a(), k == 0, k == KC - 1)
                        CP('dve' if r % 2 == 0 else 'act', vexp.ap[:, c, o0 + r * 128:o0 + (r + 1) * 128], vexp.a(c),
                           p.ap[:, 0:128], p.a())
            for c in range(4):
                nm = "xk_in%d_%d" % (l, c)
                DMA('sp', xk_in[l][c].ap()[:, 0:KXC], dram(nm), kexp.ap[:, c, :], kexp.a(c))
                DMA('sp', xk_in[l][c].ap()[:, KXC:2 * KXC], dram(nm), vexp.ap[:, c, :], vexp.a(c))
                S.coll(lambda e, c=c: e.collective_compute("AllGather", ALU.bypass, replica_groups=pairs,
                                                           ins=[xk_in[l][c].ap().opt()], outs=[xk_out[l][c].ap().opt()]),
                       reads=[dram(nm)], writes=[dram("xk_out%d_%d" % (l, c))])

        UT0 = 32768
        SS0 = 49152

        def u_proj(l):
            uT = ar(UT0, [4, TOK], BF16)
            wu = [ar(SS0 + i * 2048, [KC, 128], BF16) for i in range(2)]
            wi = winv(l)
            for a in range(4):
                w = wu[a % 2]
                DMA('pool', w.ap, w.a(), wi[:, :, 4608 + a * 128:4608 + (a + 1) * 128], dram("w_in"))
                for tt in range(NTT):
                    p = psum()
                    for k in range(KC):
                        MM(p.ap, p.a(), w.ap[:, k, :], w.a(), HT.ap[:, k, tt * TT:(tt + 1) * TT], HT.a(k), k == 0, k == KC - 1)
                    CP('act', uT.ap[:, a, tt * TT:(tt + 1) * TT], uT.a(a, (tt * TT, (tt + 1) * TT)), p.ap, p.a())
            return uT

        SC = {n: i for i, n in enumerate(["lre", "lim", "ldt", "dt", "lrdt", "mag", "th", "cos1", "sin1", "c512", "s512",
                                           "are", "aim", "am1", "den", "t1", "t2", "fre", "fim", "t3"])}

        def sc(n):
            return scol.ap[:, SC[n], :]

        def ssm_cols(l):
            DMA('sp', scol.ap[:, 0:3, :], scol.a((0, 3)), spc_d[l].rearrange("p (a b) -> p a b", a=3), dram("sp_cols"))
            A_ = scol.a()

            def tt_(o, a, b, op):
                TT_('dve', sc(o), A_, sc(a), A_, sc(b), A_, op)
            ACT(sc("dt"), A_, sc("ldt"), A_, AF.Exp)
            tt_("lrdt", "lre", "dt", ALU.mult)
            ACT(sc("mag"), A_, sc("lrdt"), A_, AF.Exp)
            tt_("th", "lim", "dt", ALU.mult)
            Y = ycol.ap
            YA = ycol.a()
            TS('dve', Y[:, 0:16], YA, sc("th"), A_, 1.0 / TWO_PI, None, ALU.mult)
            TS('dve', Y[:, 16:32], YA, Y[:, 0:16], YA, 512.0, None, ALU.mult)
            T = ytmp.ap
            TA = ytmp.a()
            TS('dve', T[:, 0, :], TA, Y, YA, MAGIC, None, ALU.add)
            STT(T[:, 1, :], TA, T[:, 0, :], TA, -MAGIC, Y, YA, ALU.add, ALU.subtract)
            ACT(T[:, 2, :], TA, T[:, 1, :], TA, AF.Sin, scale=-TWO_PI)
            TS('dve', T[:, 3, :], TA, Y, YA, 0.25, MAGIC, ALU.add, ALU.add)
            STT(T[:, 4, :], TA, T[:, 3, :], TA, -MAGIC, Y, YA, ALU.add, ALU.subtract)
            ACT(T[:, 5, :], TA, T[:, 4, :], TA, AF.Sin, bias=cst.ap[:, 1:2], scale=-TWO_PI, reads=[cst.a()])
            CP('dve', sc("sin1"), A_, T[:, 2, 0:16], TA)
            CP('dve', sc("s512"), A_, T[:, 2, 16:32], TA)
            CP('dve', sc("cos1"), A_, T[:, 5, 0:16], TA)
            CP('dve', sc("c512"), A_, T[:, 5, 16:32], TA)
            tt_("are", "mag", "cos1", ALU.mult)
            tt_("aim", "mag", "sin1", ALU.mult)
            TS('dve', sc("am1"), A_, sc("are"), A_, -1.0, None, ALU.add)
            tt_("t1", "lre", "lre", ALU.mult)
            tt_("t2", "lim", "lim", ALU.mult)
            tt_("den", "t1", "t2", ALU.add)
            S.op('dve', lambda e: e.reciprocal(out=sc("den"), in_=sc("den")), reads=[A_], writes=[A_])
            tt_("t1", "am1", "lre", ALU.mult)
            tt_("t2", "aim", "lim", ALU.mult)
            tt_("t1", "t1", "t2", ALU.add)
            tt_("fre", "t1", "den", ALU.mult)
            tt_("t1", "aim", "lre", ALU.mult)
            tt_("t2", "am1", "lim", ALU.mult)
            tt_("t1", "t1", "t2", ALU.subtract)
            tt_("fim", "t1", "den", ALU.mult)

        def ssm_prep(l, second):
            bbr = ar(SS0, [16, 128], BF16)
            bbi = ar(SS0 + 4096, [16, 128], BF16)
            cpr = ar(SS0 + 8192, [16, 128], BF16)
            cpi = ar(SS0 + 12288, [16, 128], BF16)
            dm = ar(SS0 + 16384, [4, 128], BF16)
            T0 = SS0 + 17408
            bTr = ar(T0, [16, 128], F32)
            bTi = ar(T0 + 8192, [16, 128], F32)
            dg = [ar(T0 + 16384 + i * 512, [128], F32) for i in range(2)]
            pt = [ar(T0 + 17408 + i * 2048, [512], F32) for i in range(4)]
            DMA('sp', bTr.ap, bTr.a(), bTre_d[l], dram("bT_re"))
            DMA('sp', bTi.ap, bTi.a(), bTim_d[l], dram("bT_im"))
            if second:
                DMA('pool', cpr.ap, cpr.a(), cpre_d[l], dram("Cp_re"))
                DMA('pool', cpi.ap, cpi.a(), cpim_d[l], dram("Cp_im"))
                DMA('pool', dm.ap, dm.a(), dmat_d[l], dram("dmat"))
            A_ = scol.a()
            for a in range(4):
                pF = psum()
                pG = psum()
                for q in range(4):
                    i = 4 * a + q
                    for (pp, nm) in ((pF, "fre"), (pG, "fim")):
                        d_ = dg[0] if nm == "fre" else dg[1]
                        TS('dve', d_.ap, d_.a(), ident_f.ap, ident_f.a(), scol.ap[:, SC[nm], i:i + 1], None, ALU.mult, reads=[A_])
                        MM(pp.ap[:, q * 128:(q + 1) * 128], pp.a(), ones_f.ap, ones_f.a(), d_.ap, d_.a(), True, True)
                sl = slice(4 * a, 4 * a + 4)
                br = bTr.ap[:, sl, :].rearrange("p a b -> p (a b)")
                bi = bTi.ap[:, sl, :].rearrange("p a b -> p (a b)")
                bra = bTr.a((4 * a, 4 * a + 4))
                bia = bTi.a((4 * a, 4 * a + 4))
                TT_('dve', pt[0].ap, pt[0].a(), pF.ap, pF.a(), br, bra, ALU.mult)
                TT_('dve', pt[1].ap, pt[1].a(), pG.ap, pG.a(), bi, bia, ALU.mult)
                TT_('dve', bbr.ap[:, sl, :].rearrange("p a b -> p (a b)"), bbr.a((4 * a, 4 * a + 4)), pt[0].ap, pt[0].a(),
                    pt[1].ap, pt[1].a(), ALU.subtract)
                TT_('dve', pt[2].ap, pt[2].a(), pF.ap, pF.a(), bi, bia, ALU.mult)
                TT_('dve', pt[3].ap, pt[3].a(), pG.ap, pG.a(), br, bra, ALU.mult)
                TT_('dve', bbi.ap[:, sl, :].rearrange("p a b -> p (a b)"), bbi.a((4 * a, 4 * a + 4)), pt[2].ap, pt[2].a(),
                    pt[3].ap, pt[3].a(), ALU.add)
            return bbr, bbi, cpr, cpi, dm

        def ssm_pass(l, uT, mats, second, ygT=None):
            bbr, bbi, cpr, cpi, dm = mats
            usec4 = second and not os_.environ.get('SSM_P1_OLD')
            TB = SS0 + 17408
            ct = ar(TB, [512], F32)
            st = ar(TB + 2048, [512], F32)
            ty = ar(TB + 4096, [512], F32)
            tr = ar(TB + 6144, [512], F32)
            tn = ar(TB + 8192, [512], F32)
            tn2 = tn
            T0 = TB + 10240
            names = ["bur", "bui", "p1", "p2", "p3", "p4", "bmr", "bmi", "wr", "wi"]
            tm = {n: ar(T0 + i * 2048, [512], F32) for i, n in enumerate(names)}
            for qn, pn in (("q1", "p1"), ("q2", "p2"), ("q3", "p3"), ("q4", "p4")):
                tm[qn] = tm[pn]
            xr = ar(T0 + 20480, [512], BF16)
            nxi = ar(T0 + 21504, [512], BF16)
            A_ = scol.a()
            CA = carr.a()
            import os
            for i in range(int(os.environ.get('SSM_NT', '16'))):
                a, q = divmod(i, 4)
                TS('dve', ty.ap, ty.a(), iota_f.ap, iota_f.a(), ycol.ap[:, i:i + 1], None, ALU.mult, reads=[ycol.a()])
                TS('dve', tr.ap, tr.a(), ty.ap, ty.a(), MAGIC, None, ALU.add)
                STT(tn.ap, tn.a(), tr.ap, tr.a(), -MAGIC, ty.ap, ty.a(), ALU.add, ALU.subtract)
                ACT(st.ap, st.a(), tn.ap, tn.a(), AF.Sin, scale=-TWO_PI)
                TS('dve', tr.ap, tr.a(), ty.ap, ty.a(), 0.25, MAGIC, ALU.add, ALU.add)
                STT(tn2.ap, tn2.a(), tr.ap, tr.a(), -MAGIC, ty.ap, ty.a(), ALU.add, ALU.subtract)
                ACT(ct.ap, ct.a(), tn2.ap, tn2.a(), AF.Sin, bias=cst.ap[:, 1:2], scale=-TWO_PI, reads=[cst.a()])
                for tt in range(NTT):
                    tsl = slice(tt * TT, (tt + 1) * TT)
                    ua = uT.a(a, (tt * TT, (tt + 1) * TT))
                    par = 0 if second else 2 * ((4 * i + tt) % 2)
                    pbr = PS[par]
                    pbi = PS[par + 1]
                    MM(pbr.ap, pbr.a(), bbr.ap[:, i, :], bbr.a(i), uT.ap[:, a, tsl], ua, True, True)
                    MM(pbi.ap, pbi.a(), bbi.ap[:, i, :], bbi.a(i), uT.ap[:, a, tsl], ua, True, True)

                    def pm(o, x, t, pbr=pbr, pbi=pbi):
                        if x == "bur":
                            TT_('dve', tm[o].ap, tm[o].a(), pbr.ap, pbr.a(), t.ap, t.a(), ALU.mult)
                        elif x == "bui":
                            TT_('dve', tm[o].ap, tm[o].a(), pbi.ap, pbi.a(), t.ap, t.a(), ALU.mult)
                        else:
                            TT_('dve', tm[o].ap, tm[o].a(), tm[x].ap, tm[x].a(), t.ap, t.a(), ALU.mult)
                    pm("p1", "bur", ct)
                    pm("p2", "bui", st)
                    pm("p3", "bui", ct)
                    pm("p4", "bur", st)
                    TT_('dve', tm["bmr"].ap, tm["bmr"].a(), tm["p1"].ap, tm["p1"].a(), tm["p2"].ap, tm["p2"].a(), ALU.add)
                    TT_('dve', tm["bmi"].ap, tm["bmi"].a(), tm["p3"].ap, tm["p3"].a(), tm["p4"].ap, tm["p4"].a(), ALU.subtract)
                    magb = scol.ap[:, SC["mag"], i:i + 1].to_broadcast([128, TT])
                    for (w_, b_, cc) in (("wr", "bmr", i), ("wi", "bmi", 16 + i)):
                        wt, bt = tm[w_], tm[b_]
                        S.op('dve', lambda e, wt=wt, bt=bt, cc=cc, magb=magb, tt=tt: e.tensor_tensor_scan(
                            out=wt.ap, data0=magb, data1=bt.ap, initial=(carr4.ap[:, tt, cc:cc + 1] if usec4 else carr.ap[:, cc:cc + 1]), op0=ALU.mult, op1=ALU.add),
                            reads=[bt.a(), A_, CA, carr4.a()], writes=[wt.a()])
                    if not usec4:
                        wrl = tm["wr"].ap[:, TT - 1:TT]
                        wil = tm["wi"].ap[:, TT - 1:TT]
                        c5 = scol.ap[:, SC["c512"], i:i + 1]
                        s5 = scol.ap[:, SC["s512"], i:i + 1]
                        TT_('dve', ctmp.ap[:, 0:1], ctmp.a(), wil, tm["wi"].a(), s5, A_, ALU.mult)
                        TT_('dve', ctmp.ap[:, 1:2], ctmp.a(), wrl, tm["wr"].a(), s5, A_, ALU.mult)
                        STT(carr.ap[:, i:i + 1], CA, wrl, tm["wr"].a(), c5, ctmp.ap[:, 0:1], ctmp.a(), ALU.mult, ALU.subtract, reads=[A_])
                        STT(carr.ap[:, 16 + i:17 + i], CA, wil, tm["wi"].a(), c5, ctmp.ap[:, 1:2], ctmp.a(), ALU.mult, ALU.add, reads=[A_])
                    if second:
                        qb = [ar(T0 + (2 + k_) * 2048, [512], BF16) for k_ in range(4)]
                        TT_('dve', qb[0].ap, qb[0].a(), tm["wr"].ap, tm["wr"].a(), ct.ap, ct.a(), ALU.mult)
                        STT(qb[1].ap, qb[1].a(), tm["wi"].ap, tm["wi"].a(), -1.0, st.ap, st.a(), ALU.mult, ALU.mult)
                        STT(qb[2].ap, qb[2].a(), tm["wi"].ap, tm["wi"].a(), -1.0, ct.ap, ct.a(), ALU.mult, ALU.mult)
                        STT(qb[3].ap, qb[3].a(), tm["wr"].ap, tm["wr"].a(), -1.0, st.ap, st.a(), ALU.mult, ALU.mult)
                        py = PS[2 + tt]
                        if q == 0:
                            MM(py.ap, py.a(), dm.ap[:, a, :], dm.a(a), uT.ap[:, a, tsl], ua, True, False)
                        MM(py.ap, py.a(), cpr.ap[:, i, :], cpr.a(i), qb[0].ap, qb[0].a(), False, False)
                        MM(py.ap, py.a(), cpr.ap[:, i, :], cpr.a(i), qb[1].ap, qb[1].a(), False, False)
                        MM(py.ap, py.a(), cpi.ap[:, i, :], cpi.a(i), qb[2].ap, qb[2].a(), False, False)
                        MM(py.ap, py.a(), cpi.ap[:, i, :], cpi.a(i), qb[3].ap, qb[3].a(), False, q == 3)
                        if q == 3:
                            ACT(ygT.ap[:, a, tsl], ygT.a(a, (tt * TT, (tt + 1) * TT)), py.ap, py.a(), AF.Gelu_apprx_tanh)

        def ssm_pass1_fast(l, uT, mats):
            bbr, bbi, cpr, cpi, dm = mats
            TB = SS0 + 17408
            ct = ar(TB, [512], F32)
            st = ar(TB + 2048, [512], F32)
            ty = ar(TB + 4096, [512], F32)
            tr = ar(TB + 6144, [512], F32)
            tn = ar(TB + 8192, [512], F32)
            T0 = TB + 10240
            junk = ar(T0, [512], F32)
            wre = ar(T0 + 2048, [512], F32)
            wim = ar(T0 + 4096, [512], F32)
            mp = ar(T0 + 6144, [512], F32)
            rio = ar(T0 + 8192, [512], F32)
            A_ = scol.a()
            TS('dve', rio.ap, rio.a(), iota_f.ap, iota_f.a(), -1.0, 511.0, ALU.mult, ALU.add)
            for i in range(16):
                a, q = divmod(i, 4)
                TS('dve', ty.ap, ty.a(), iota_f.ap, iota_f.a(), ycol.ap[:, i:i + 1], None, ALU.mult, reads=[ycol.a()])
                TS('dve', tr.ap, tr.a(), ty.ap, ty.a(), MAGIC, None, ALU.add)
                STT(tn.ap, tn.a(), tr.ap, tr.a(), -MAGIC, ty.ap, ty.a(), ALU.add, ALU.subtract)
                ACT(st.ap, st.a(), tn.ap, tn.a(), AF.Sin, scale=-TWO_PI)
                TS('dve', tr.ap, tr.a(), ty.ap, ty.a(), 0.25, MAGIC, ALU.add, ALU.add)
                STT(tn.ap, tn.a(), tr.ap, tr.a(), -MAGIC, ty.ap, ty.a(), ALU.add, ALU.subtract)
                ACT(ct.ap, ct.a(), tn.ap, tn.a(), AF.Sin, bias=cst.ap[:, 1:2], scale=-TWO_PI, reads=[cst.a()])
                ACT(mp.ap, mp.a(), rio.ap, rio.a(), AF.Exp, scale=scol.ap[:, SC["lrdt"], i:i + 1], reads=[A_])
                TT_('dve', wre.ap, wre.a(), mp.ap, mp.a(), ct.ap, ct.a(), ALU.mult)
                TT_('dve', wim.ap, wim.a(), mp.ap, mp.a(), st.ap, st.a(), ALU.mult)
                for tt in range(NTT):
                    tsl = slice(tt * TT, (tt + 1) * TT)
                    ua = uT.a(a, (tt * TT, (tt + 1) * TT))
                    par = 2 * ((4 * i + tt) % 2)
                    pbr = PS[par]
                    pbi = PS[par + 1]
                    MM(pbr.ap, pbr.a(), bbr.ap[:, i, :], bbr.a(i), uT.ap[:, a, tsl], ua, True, True)
                    MM(pbi.ap, pbi.a(), bbi.ap[:, i, :], bbi.a(i), uT.ap[:, a, tsl], ua, True, True)
                    for k_, (pp, ww) in enumerate(((pbr, wre), (pbi, wim), (pbi, wre), (pbr, wim))):
                        S.op('dve', lambda e, pp=pp, ww=ww, k_=k_, tt=tt, i=i: e.scalar_tensor_tensor(
                            out=junk.ap, in0=pp.ap, scalar=1.0, in1=ww.ap, op0=ALU.mult, op1=ALU.mult,
                            accum_out=psums.ap[:, k_, tt, i:i + 1]),
                            reads=[pp.a(), ww.a()], writes=[junk.a(), psums.a()])
            CC = ccol.a()

            def cc(n):
                return ccol.ap[:, n, :]
            ACT(cc(0), CC, scol.ap[:, SC["lrdt"], :], A_, AF.Exp, scale=512.0)
            S.op('dve', lambda e: e.memset(ccol.ap[:, 1:3, :], 0.0), writes=[CC])
            c5 = scol.ap[:, SC["c512"], :]
            s5 = scol.ap[:, SC["s512"], :]
            PA = psums.a()
            for tt in range(NTT):
                TT_('dve', cc(3), CC, psums.ap[:, 0, tt, :], PA, psums.ap[:, 1, tt, :], PA, ALU.add)
                TT_('dve', cc(4), CC, psums.ap[:, 2, tt, :], PA, psums.ap[:, 3, tt, :], PA, ALU.subtract)
                TT_('dve', cc(5), CC, cc(1), CC, cc(0), CC, ALU.mult)
                TT_('dve', cc(5), CC, cc(5), CC, cc(3), CC, ALU.add)
                TT_('dve', cc(6), CC, cc(2), CC, cc(0), CC, ALU.mult)
                TT_('dve', cc(6), CC, cc(6), CC, cc(4), CC, ALU.add)
                TT_('dve', cc(7), CC, cc(5), CC, c5, A_, ALU.mult)
                TT_('dve', cc(8), CC, cc(6), CC, s5, A_, ALU.mult)
                TT_('dve', cc(1), CC, cc(7), CC, cc(8), CC, ALU.subtract)
                TT_('dve', cc(7), CC, cc(6), CC, c5, A_, ALU.mult)
                TT_('dve', cc(8), CC, cc(5), CC, s5, A_, ALU.mult)
                TT_('dve', cc(2), CC, cc(7), CC, cc(8), CC, ALU.add)
            CP('dve', carr.ap[:, 0:16], carr.a(), cc(1), CC)
            CP('dve', carr.ap[:, 16:32], carr.a(), cc(2), CC)

        def carry_exchange(l):
            nm = "sx_in%d" % l
            DMA('sp', sx_in[l].ap(), dram(nm), carr.ap, carr.a())
            S.coll(lambda e: e.collective_compute("AllGather", ALU.bypass, replica_groups=pairs,
                                                  ins=[sx_in[l].ap().opt()], outs=[sx_out[l].ap().opt()]),
                   reads=[dram(nm)], writes=[dram("sx_out%d" % l)])
            DMA('sp', cimp.ap, cimp.a(), sx_out[l].ap()[0:128, :], dram("sx_out%d" % l))
            TS('dve', carr.ap, carr.a(), cimp.ap, cimp.a(), cst.ap[:, 2:3], None, ALU.mult, reads=[cst.a()])

        def carry_chain(l):
            CC = ccol.a()
            A_ = scol.a()
            PA = psums.a()
            C4 = carr4.a()

            def cc(n):
                return ccol.ap[:, n, :]
            c5 = scol.ap[:, SC["c512"], :]
            s5 = scol.ap[:, SC["s512"], :]
            CP('dve', cc(1), CC, carr.ap[:, 0:16], carr.a())
            CP('dve', cc(2), CC, carr.ap[:, 16:32], carr.a())
            for tt in range(NTT):
                CP('dve', carr4.ap[:, tt, 0:16], C4, cc(1), CC)
                CP('dve', carr4.ap[:, tt, 16:32], C4, cc(2), CC)
                if tt == NTT - 1:
                    break
                TT_('dve', cc(3), CC, psums.ap[:, 0, tt, :], PA, psums.ap[:, 1, tt, :], PA, ALU.add)
                TT_('dve', cc(4), CC, psums.ap[:, 2, tt, :], PA, psums.ap[:, 3, tt, :], PA, ALU.subtract)
                TT_('dve', cc(5), CC, cc(1), CC, cc(0), CC, ALU.mult)
                TT_('dve', cc(5), CC, cc(5), CC, cc(3), CC, ALU.add)
                TT_('dve', cc(6), CC, cc(2), CC, cc(0), CC, ALU.mult)
                TT_('dve', cc(6), CC, cc(6), CC, cc(4), CC, ALU.add)
                TT_('dve', cc(7), CC, cc(5), CC, c5, A_, ALU.mult)
                TT_('dve', cc(8), CC, cc(6), CC, s5, A_, ALU.mult)
                TT_('dve', cc(1), CC, cc(7), CC, cc(8), CC, ALU.subtract)
                TT_('dve', cc(7), CC, cc(6), CC, c5, A_, ALU.mult)
                TT_('dve', cc(8), CC, cc(5), CC, s5, A_, ALU.mult)
                TT_('dve', cc(2), CC, cc(7), CC, cc(8), CC, ALU.add)

        def attention(l):
            oT = ar(0, [4, TOK], BF16)
            R0 = 32768
            wq = [ar(R0 + i * 2048, [KC, 128], BF16) for i in range(2)]
            wk = [ar(R0 + 4096 + i * 2048, [KC, 128], BF16) for i in range(2)]
            wv = [ar(R0 + 8192 + i * 2048, [KC, 128], BF16) for i in range(2)]
            qT = ar(R0 + 12288, [TOK], BF16)
            kT = ar(R0 + 16384, [TOK], BF16)
            vB = ar(R0 + 20480, [16, 128], BF16)
            kh = ar(R0 + 24576, [TOK], BF16)
            vh = ar(R0 + 28672, [16, 128], BF16)
            Oacc = ar(R0 + 32768, [TOK], F32)
            Dacc = ar(R0 + 40960, [TOK], F32)
            bN = ar(R0 + 49152, [6, 256], F32)
            bF = ar(R0 + 55296, [6, 256], F32)
            Sf = [ar(R0 + 61440 + i * 1024, [256], F32) for i in range(3)]
            Pb = [ar(R0 + 64512 + i * 512, [256], BF16) for i in range(3)]
            PTs = [ar(R0 + 66048 + i * 512, [2, 128], BF16) for i in range(3)]
            wi = winv(l)
            it = 0
            uc = 0
            import os
            AST = int(os.environ.get('ATT_STAGE', '6'))
            for c in range(int(os.environ.get('ATT_C', '4'))):
                for g in range(3):
                    h0 = g * 8 + 2 * c
                    DMA('sp', bN.ap[:, 2 * g:2 * g + 2, :], bN.a((2 * g, 2 * g + 2)), biasg_d[:, h0:h0 + 2, :], dram("biasg"))
                for j in range(6):
                    TT_('dve', bN.ap[:, j, :], bN.a(j), bN.ap[:, j, :], bN.a(j), maskc.ap, maskc.a(), ALU.add)
                    CP('dve', bF.ap[:, j, 128:256], bF.a(j), bN.ap[:, j, 128:256], bN.a(j))
                    TT_('dve', bF.ap[:, j, 0:128], bF.a(j), bN.ap[:, j, 0:128], bN.a(j), halom.ap, halom.a(), ALU.add)
                S.op('pool', lambda e: e.memset(Oacc.ap, 0.0), writes=[Oacc.a()])
                S.op('pool', lambda e: e.memset(Dacc.ap, 0.0), writes=[Dacc.a()])
                for g in range(int(os.environ.get('ATT_G', '3'))):
                    d = GROUPS[g][1]
                    Lc = TOK // d
                    nb = Lc // 128
                    qcol = g * 512 + c * 128
                    wq_, wk_, wv_ = wq[it % 2], wk[it % 2], wv[it % 2]
                    it += 1
                    DMA('pool', wq_.ap, wq_.a(), wi[:, :, qcol:qcol + 128], dram("w_in"))
                    DMA('pool', wk_.ap, wk_.a(), wi[:, :, 1536 + qcol:1536 + qcol + 128], dram("w_in"))
                    DMA('pool', wv_.ap, wv_.a(), wi[:, :, 3072 + qcol:3072 + qcol + 128], dram("w_in"))
                    o0 = GOFFS[g]
                    xo = xk_out[l][c].ap()
                    xon = dram("xk_out%d_%d" % (l, c))
                    DMA('sp', kh.ap[:, 0:d * 128], kh.a(), xo[0:128, o0:o0 + d * 128], xon)
                    DMA('sp', vh.ap[:, 0:d, :].rearrange("p a b -> p (a b)"), vh.a(),
                        xo[0:128, KXC + o0:KXC + o0 + d * 128], xon)
                    for (w_, dst, eng) in (((wq_, qT, 'act'), (wk_, kT, 'dve')) if AST >= 2 else ()):
                        for tt in range(NTT):
                            p = psum()
                            for k in range(KC):
                                MM(p.ap, p.a(), w_.ap[:, k, :], w_.a(), HT.ap[:, k, tt * TT:(tt + 1) * TT], HT.a(k),
                                   k == 0, k == KC - 1)
                            if d == 1:
                                o_ap = dst.ap[:, tt * TT:(tt + 1) * TT]
                                i_ap = p.ap
                            else:
                                jn = TT // d
                                o_ap = dst.ap.rearrange("p (r j) -> p r j", r=d)[:, :, jn * tt:jn * (tt + 1)]
                                i_ap = p.ap.rearrange("p (j r) -> p r j", r=d)
                            CP(eng, o_ap, dst.a(), i_ap, p.a())
                    for blk in (range(16) if AST >= 3 else ()):
                        _, _, cols = blk_tokens(g, blk)
                        p = psum()
                        for k in range(KC):
                            MM(p.ap[:, 0:128], p.a(), HT.ap[:, k, cols], HT.a(k), wv_.ap[:, k, :], wv_.a(), k == 0, k == KC - 1)
                        CP('act' if blk % 2 == 0 else 'dve', vB.ap[:, blk, :], vB.a(blk), p.ap[:, 0:128], p.a())
                    units = [(blk, hh) for blk in range(16) for hh in range(2)]
                    NU = len(units)

                    def uinfo(k):
                        blk, hh = units[k]
                        r, n, cols = blk_tokens(g, blk)
                        return blk, hh, r, n, cols, (n == 0), r * Lc + 128 * n, slice(64 * hh, 64 * hh + 64)

                    def stA(k):
                        blk, hh, r, n, cols, first, q0, pr = uinfo(k)
                        pS = PS[4 + k % 2]
                        if first:
                            kprev, kpa = kh.ap[pr, r * 128:(r + 1) * 128], kh.a()
                        else:
                            kprev, kpa = kT.ap[pr, q0 - 128:q0], kT.a()
                        if first:
                            MM(pS.ap[:, 0:128], pS.a(), qT.ap[pr, q0:q0 + 128], qT.a(), kprev, kpa, True, True)
                            MM(pS.ap[:, 128:256], pS.a(), qT.ap[pr, q0:q0 + 128], qT.a(), kT.ap[pr, q0:q0 + 128], kT.a(), True, True)
                        else:
                            MM(pS.ap[:, 0:256], pS.a(), qT.ap[pr, q0:q0 + 128], qT.a(), kT.ap[pr, q0 - 128:q0 + 128], kT.a(), True, True)
                        bt = bF if first else bN
                        u3 = k % 3
                        STT(Sf[u3].ap, Sf[u3].a(), pS.ap[:, 0:256], pS.a(), 0.125, bt.ap[:, 2 * g + hh, :], bt.a(2 * g + hh), ALU.mult, ALU.add)
                        ACT(Pb[u3].ap, Pb[u3].a(), Sf[u3].ap, Sf[u3].a(), AF.Exp)

                    def stB(k):
                        u3 = k % 3
                        pT = PSB[k % 2]
                        TR(pT.ap[:, 0:128], pT.a(), Pb[u3].ap[:, 0:128], Pb[u3].a())
                        TR(pT.ap[:, 128:256], pT.a(), Pb[u3].ap[:, 128:256], Pb[u3].a())
                        CP('act', PTs[u3].ap.rearrange("p a b -> p (a b)"), PTs[u3].a(), pT.ap[:, 0:256], pT.a())

                    def stC(k):
                        blk, hh, r, n, cols, first, q0, pr = uinfo(k)
                        u3 = k % 3
                        po = PS[2 * (blk % 2)]
                        pd = PS[2 * (blk % 2) + 1]
                        if first:
                            vprev, vpa = vh.ap[:, r, 64 * hh:64 * hh + 64], vh.a()
                        else:
                            vprev, vpa = vB.ap[:, blk - 1, 64 * hh:64 * hh + 64], vB.a(blk - 1)
                        MM(po.ap[pr, 0:128], po.a(), vprev, vpa, PTs[u3].ap[:, 0, :], PTs[u3].a(), True, False)
                        MM(po.ap[pr, 0:128], po.a(), vB.ap[:, blk, 64 * hh:64 * hh + 64], vB.a(blk), PTs[u3].ap[:, 1, :],
                           PTs[u3].a(), False, True)
                        MM(pd.ap[pr, 0:128], pd.a(), ones_b.ap[:, 0:64], ones_b.a(), PTs[u3].ap[:, 0, :], PTs[u3].a(), True, False)
                        MM(pd.ap[pr, 0:128], pd.a(), ones_b.ap[:, 0:64], ones_b.a(), PTs[u3].ap[:, 1, :], PTs[u3].a(), False, True)
                        if hh == 1:
                            STT(Oacc.ap[:, cols], Oacc.a(), po.ap[:, 0:128], po.a(), 1.0, Oacc.ap[:, cols], Oacc.a(), ALU.mult, ALU.add)
                            STT(Dacc.ap[:, cols], Dacc.a(), pd.ap[:, 0:128], pd.a(), 1.0, Dacc.ap[:, cols], Dacc.a(), ALU.mult, ALU.add)

                    for k in range(NU + 2):
                        if k < NU:
                            stA(k)
                        if 1 <= k <= NU:
                            stB(k - 1)
                        if 2 <= k <= NU + 1:
                            stC(k - 2)
                S.op('dve', lambda e: e.reciprocal(out=Dacc.ap, in_=Dacc.ap), reads=[Dacc.a()], writes=[Dacc.a()])
                TT_('dve', oT.ap[:, c, :], oT.a(c), Oacc.ap, Oacc.a(), Dacc.ap, Dacc.a(), ALU.mult)
            return oT

        def mix_tail(l, oT, ygT):
            R0 = 32768
            mix = ar(R0, [KC, TOK], BF16)
            wgl = [ar(R0 + 32768 + i * 2048, [4, 256], BF16) for i in range(2)]
            wgt = [ar(R0 + 36864 + i * 4096, [KC, 256], BF16) for i in range(2)]
            wap = [ar(R0 + 45056 + i * 1024, [4, 128], BF16) for i in range(2)]
            wou = [ar(R0 + 47104 + i * 2048, [KC, 128], BF16) for i in range(2)]
            tmp = [ar(R0 + 51200 + i * 2048, [512], F32) for i in range(5)]
            wi = winv(l)
            wg_d = w_glu_d[l].rearrange("(kc p) n -> p kc n", p=128)
            wa_d = w_ap_d[l].rearrange("(kc p) n -> p kc n", p=128)
            wo_d = w_out_d[l].rearrange("(kc p) n -> p kc n", p=128)
            for m in range(KC):
                g_, t_, a_ = wgl[m % 2], wgt[m % 2], wap[m % 2]
                DMA('pool', g_.ap[:, :, 0:128], g_.a(), wg_d[:, :, m * 128:(m + 1) * 128], dram("w_glu"))
                DMA('pool', g_.ap[:, :, 128:256], g_.a(), wg_d[:, :, 1024 + m * 128:1024 + (m + 1) * 128], dram("w_glu"))
                DMA('pool', t_.ap[:, :, 0:128], t_.a(), wi[:, :, 5120 + m * 128:5120 + (m + 1) * 128], dram("w_in"))
                DMA('pool', t_.ap[:, :, 128:256], t_.a(), wi[:, :, 6144 + m * 128:6144 + (m + 1) * 128], dram("w_in"))
                DMA('pool', a_.ap, a_.a(), wa_d[:, :, m * 128:(m + 1) * 128], dram("w_attn_proj"))
                for tt in range(NTT):
                    tsl = slice(tt * TT, (tt + 1) * TT)
                    pga, pgb, pgs, pgt, pya = psum(), psum(), psum(), psum(), psum()
                    for k in range(4):
                        MM(pga.ap, pga.a(), g_.ap[:, k, 0:128], g_.a(), ygT.ap[:, k, tsl], ygT.a(k), k == 0, k == 3)
                    for k in range(4):
                        MM(pgb.ap, pgb.a(), g_.ap[:, k, 128:256], g_.a(), ygT.ap[:, k, tsl], ygT.a(k), k == 0, k == 3)
                    for k in range(KC):
                        MM(pgt.ap, pgt.a(), t_.ap[:, k, 0:128], t_.a(), HT.ap[:, k, tsl], HT.a(k), k == 0, k == KC - 1)
                    for k in range(KC):
                        MM(pgs.ap, pgs.a(), t_.ap[:, k, 128:256], t_.a(), HT.ap[:, k, tsl], HT.a(k), k == 0, k == KC - 1)
                    for k in range(4):
                        MM(pya.ap, pya.a(), a_.ap[:, k, :], a_.a(), oT.ap[:, k, tsl], oT.a(k), k == 0, k == 3)
                    ACT(tmp[0].ap, tmp[0].a(), pgb.ap, pgb.a(), AF.Sigmoid)
                    ACT(tmp[1].ap, tmp[1].a(), pgs.ap, pgs.a(), AF.Sigmoid)
                    ACT(tmp[2].ap, tmp[2].a(), pgt.ap, pgt.a(), AF.Sigmoid)
                    TT_('dve', tmp[3].ap, tmp[3].a(), pga.ap, pga.a(), tmp[0].ap, tmp[0].a(), ALU.mult)
                    TT_('dve', tmp[3].ap, tmp[3].a(), tmp[3].ap, tmp[3].a(), tmp[1].ap, tmp[1].a(), ALU.mult)
                    TT_('dve', tmp[4].ap, tmp[4].a(), pya.ap, pya.a(), tmp[2].ap, tmp[2].a(), ALU.mult)
                    TT_('dve', mix.ap[:, m, tsl], mix.a(m, (tt * TT, (tt + 1) * TT)), tmp[3].ap, tmp[3].a(), tmp[4].ap, tmp[4].a(), ALU.add)
            for m in range(KC):
                w_ = wou[m % 2]
                DMA('pool', w_.ap, w_.a(), wo_d[:, :, m * 128:(m + 1) * 128], dram("w_out"))
                for tt in range(NTT):
                    tsl = slice(tt * TT, (tt + 1) * TT)
                    p = psum()
                    for k in range(KC):
                        MM(p.ap, p.a(), w_.ap[:, k, :], w_.a(), mix.ap[:, k, tsl], mix.a(k, (tt * TT, (tt + 1) * TT)), k == 0, k == KC - 1)
                    xa = XT.a(m, (tt * TT, (tt + 1) * TT))
                    STT(XT.ap[:, m, tsl], xa, p.ap, p.a(), modT.ap[:, l, 40 + m:41 + m], XT.ap[:, m, tsl], xa, ALU.mult, ALU.add,
                        reads=[modT.a(l)])

        def dump(name):
            if name in dbg_d:
                for m in range(KC):
                    DMA('sp', dbg_d[name][m * 128:(m + 1) * 128, :], dram("dbg_" + name), XT.ap[:, m, :], XT.a(m))

        def dump_bf(name, tl, nchunk):
            if name in dbg_d:
                tmpf = ar(81920, [TOK], F32)
                for m in range(nchunk):
                    CP('dve', tmpf.ap, tmpf.a(), tl.ap[:, m, :], tl.a(m))
                    DMA('sp', dbg_d[name][m * 128:(m + 1) * 128, :], dram("dbg_" + name), tmpf.ap, tmpf.a())

        done = False
        for l in range(n_layers):
            norm_mod(l, 0)
            ffn(l, 0)
            dump("x1_%d" % l)
            if stop_after == "ffn1":
                done = True
                break
            norm_mod(l, 1)
            halo_export(l)
            if stop_after == "halo":
                done = True
                break
            S.op('pool', lambda e: e.memset(carr.ap, 0.0), writes=[carr.a()])
            ssm_cols(l)
            uT = u_proj(l)
            mats = ssm_prep(l, False)
            if stop_after == "prep":
                done = True
                break
            if os_.environ.get('SSM_P1_OLD'):
                ssm_pass(l, uT, mats, False)
            else:
                ssm_pass1_fast(l, uT, mats)
            carry_exchange(l)
            fastc = not os_.environ.get('SSM_P1_OLD')
            if fastc:
                carry_chain(l)
            if stop_after == "ssm1":
                done = True
                break
            oT = attention(l)
            dump_bf("oT_%d" % l, oT, 4)
            if stop_after == "attn":
                done = True
                break
            uT = u_proj(l)
            mats = ssm_prep(l, True)
            ygT = ar(16384, [4, TOK], BF16)
            ssm_pass(l, uT, mats, True, ygT)
            if stop_after == "ssm2":
                done = True
                break
            dump_bf("yg_%d" % l, ygT, 4)
            mix_tail(l, oT, ygT)
            dump("x2_%d" % l)
            if stop_after == "mix":
                done = True
                break
            norm_mod(l, 2)
            ffn(l, 1)
            dump("x3_%d" % l)
        if not done:
            norm(lambda m: (normsT.ap[:, 48 + m:49 + m], normsT.a()), None, False)
        fin = [dram("outT")]
        for m in range(KC):
            DMA('sp', out_d[m * 128:(m + 1) * 128, :], dram("outT"), XT.ap[:, m, :], XT.a(m))
        for name in dbg_d:
            fin.append(dram("dbg_" + name))
        S.finish_wait('sp', fin)
        S.emit()
    return nc


def _t5_bucket(dist):
    max_exact = 16
    dd = np.maximum(dist, max_exact).astype(np.float32)
    large = max_exact + (np.log(dd / max_exact) / np.log(2048 / max_exact) * (32 - max_exact)).astype(np.int32)
    large = np.minimum(large, 31)
    return np.where(dist < max_exact, dist, large).astype(np.int32)


def prep_shared(inp):
    f = np.float32
    sh = {}
    for k in ("w_ffn1_in", "w_ffn1_out", "w_ffn2_in", "w_ffn2_out", "w_in", "w_glu", "w_attn_proj", "w_out"):
        sh[k] = np.ascontiguousarray(inp[k], dtype=f)
    sh["b_adaT"] = np.ascontiguousarray(inp["b_ada"].reshape(2, 72, 128).transpose(0, 2, 1), dtype=f)
    norms = np.zeros((128, 56), f)
    for l in range(2):
        for wi, nm in enumerate(("norm_ffn1", "norm_mix", "norm_ffn2")):
            norms[:, l * 24 + wi * 8:l * 24 + wi * 8 + 8] = inp[nm][l].reshape(8, 128).T
    norms[:, 48:56] = inp["final_norm"].reshape(8, 128).T
    sh["normsT"] = norms
    qi = np.arange(128)[:, None]
    kj = np.arange(256)[None, :]
    rel = 128 + qi - kj
    band = (rel >= 0) & (rel <= 128)
    biasg = np.zeros((128, 24, 256), f)
    for g, (window, dil) in enumerate(GROUPS):
        bucket = _t5_bucket(np.clip(rel, 0, None) * dil)
        for h in range(8):
            biasg[:, g * 8 + h, :] = inp["rel_bias"][bucket, g * 8 + h]
    sh["biasg"] = biasg
    sh["maskc"] = np.where(band, 0.0, NEG).astype(f)
    spc = np.zeros((2, 128, 48), f)
    bTr = np.zeros((2, 128, 16, 128), f)
    bTi = np.zeros((2, 128, 16, 128), f)
    cpr = np.zeros((2, 128, 16, 128), f)
    cpi = np.zeros((2, 128, 16, 128), f)
    dm = np.zeros((2, 128, 4, 128), f)
    for l in range(2):
        for i in range(16):
            for gl in range(2):
                g = 2 * i + gl
                ps_ = slice(gl * 64, gl * 64 + 64)
                spc[l, ps_, i] = inp["lam_re"][l, g]
                spc[l, ps_, 16 + i] = inp["lam_im"][l, g]
                spc[l, ps_, 32 + i] = inp["log_dt"][l, g]
                ch0 = (i % 4) * 32 + gl * 16
                bTr[l, ch0:ch0 + 16, i, ps_] = inp["b_re"][l, g].T
                bTi[l, ch0:ch0 + 16, i, ps_] = inp["b_im"][l, g].T
                cpr[l, ps_, i, ch0:ch0 + 16] = inp["c_re"][l, g].T
                cpi[l, ps_, i, ch0:ch0 + 16] = inp["c_im"][l, g].T
        dflat = inp["d_skip"][l].reshape(512)
        for a in range(4):
            dm[l, np.arange(128), a, np.arange(128)] = dflat[a * 128:(a + 1) * 128]
    sh["sp_cols"] = spc
    sh["bT_re"] = bTr
    sh["bT_im"] = bTi
    sh["Cp_re"] = cpr
    sh["Cp_im"] = cpi
    sh["dmat"] = dm
    sh["ident"] = np.eye(128, dtype=f)
    sh["iota"] = np.ascontiguousarray(np.broadcast_to(np.arange(512, dtype=f)[None, :], (128, 512)))
    return sh


def prep_core(inp, sh, core):
    f = np.float32
    b, half = divmod(core, 2)
    m = dict(sh)
    m["xT"] = np.ascontiguousarray(inp["x"][b, half * TOK:(half + 1) * TOK, :].T, dtype=f)
    m["cT"] = np.ascontiguousarray(inp["c"][b].reshape(8, 128).T, dtype=f)
    m["w_ada_h"] = np.ascontiguousarray(inp["w_ada"][:, :, half * 4608:(half + 1) * 4608], dtype=f)
    m["halom"] = np.full((128, 128), NEG if half == 0 else 0.0, f)
    m["flagb"] = np.full((128, 1), 0.0 if half == 0 else 1.0, f)
    return m


_NC = None


def kernel(**inputs):
    global _NC
    inp = {k: np.asarray(v) for k, v in inputs.items()}
    sh = prep_shared(inp)
    in_maps = [prep_core(inp, sh, c) for c in range(8)]
    if _NC is None:
        _NC = build()
    res = run_bass_kernel_spmd(_NC, in_maps, core_ids=list(range(8)))
    out = np.empty((4, 4096, D), np.float32)
    for c in range(8):
        b, half = divmod(c, 2)
        out[b, half * TOK:(half + 1) * TOK, :] = res.results[c]["outT"].T
    return out
```

```python
import math
import os as os_
from contextlib import ExitStack

import numpy as np
import concourse.bass as bass
import concourse.mybir as mybir
from concourse.bass_utils import run_bass_kernel_spmd

F32 = mybir.dt.float32
BF16 = mybir.dt.bfloat16
AF = mybir.ActivationFunctionType
ALU = mybir.AluOpType

D = 1024
KC = 8
TOK = 2048
NTT = 4
TT = 512
DFF = 2816
FC = 22
GROUPS = ((128, 1), (512, 4), (2048, 16))
MAGIC = 12582912.0
TWO_PI = 2.0 * math.pi
NEG = -30000.0
INF = 1 << 40
ARENA = 100352
NKX = 21504
KXC = 2688
GOFFS = (0, 128, 640)


class Sched:
    ENG = ('pe', 'act', 'dve', 'pool', 'sp')

    LIMIT = 2000

    def __init__(self, nc, es, n_dma=6, same_engine_sync=True):
        self.nc = nc
        self.es = es
        self.same = same_engine_sync
        self.ops = {e: [] for e in self.ENG}
        self.sem = {}
        self.cnt = {}
        self.cur = {}
        self.epoch = {}
        for e in self.ENG:
            self._new_sem(e)
        self.dma_keys = {}
        for q in ('sp', 'pool'):
            ks = []
            for i in range(n_dma):
                k = "d_%s%d" % (q, i)
                self._new_sem(k)
                ks.append(k)
            self.dma_keys[q] = ks
        self.dma_rr = {q: 0 for q in self.dma_keys}
        self.waited = {e: {} for e in self.ENG}
        self.res = {}

    def _new_sem(self, stream):
        ep = self.epoch.get(stream, -1) + 1
        self.epoch[stream] = ep
        k = "%s#%d" % (stream, ep)
        self.sem[k] = self.es.enter_context(self.nc.semaphore("s_%s_%d" % (stream, ep)))
        self.cnt[k] = 0
        self.cur[stream] = k
        return k

    def _cover(self, acc):
        name, lo, hi = acc
        segs = self.res.get(name)
        if segs is None:
            segs = [[0, INF, {}, {}]]
        out = []
        new = []
        for s in segs:
            slo, shi = s[0], s[1]
            if shi <= lo or slo >= hi:
                new.append(s)
                continue
            if slo < lo:
                new.append([slo, lo, dict(s[2]), dict(s[3])])
            if shi > hi:
                new.append([hi, shi, dict(s[2]), dict(s[3])])
            s[0] = max(slo, lo)
            s[1] = min(shi, hi)
            new.append(s)
            out.append(s)
        self.res[name] = new
        return out

    def _deps(self, reads, writes):
        deps = {}
        rsegs = [s for a in reads for s in self._cover(a)]
        wsegs = [s for a in writes for s in self._cover(a)]
        for s in rsegs:
            for sk, v in s[2].items():
                if deps.get(sk, 0) < v:
                    deps[sk] = v
        for s in wsegs:
            for dd in (s[2], s[3]):
                for sk, v in dd.items():
                    if deps.get(sk, 0) < v:
                        deps[sk] = v
        return deps, rsegs, wsegs

    @staticmethod
    def _record(rsegs, wsegs, sk, v):
        for s in rsegs:
            if s[3].get(sk, 0) < v:
                s[3][sk] = v
        for s in wsegs:
            s[2] = {sk: v}
            s[3] = {}

    def op(self, eng, fn, reads=(), writes=()):
        deps, rsegs, wsegs = self._deps(reads, writes)
        waits = []
        for sk, v in deps.items():
            if sk.split('#')[0] == eng and (eng == 'pe' or not self.same):
                continue
            if self.waited[eng].get(sk, 0) < v:
                waits.append((sk, v))
                self.waited[eng][sk] = v
        k = self.cur[eng]
        if self.cnt[k] >= self.LIMIT:
            k = self._new_sem(eng)
        self.cnt[k] += 1
        self.ops[eng].append((waits, fn, k, 1))
        self._record(rsegs, wsegs, k, self.cnt[k])

    def dma(self, q, fn, reads=(), writes=()):
        deps, rsegs, wsegs = self._deps(reads, writes)
        ks = self.dma_keys[q]
        stream = ks[self.dma_rr[q] % len(ks)]
        self.dma_rr[q] += 1
        dk = self.cur[stream]
        if self.cnt[dk] > 0 and deps.get(dk, 0) < self.cnt[dk]:
            deps[dk] = self.cnt[dk]
        waits = []
        for sk, v in deps.items():
            if self.waited[q].get(sk, 0) < v:
                waits.append((sk, v))
                self.waited[q][sk] = v
        if self.cnt[dk] >= self.LIMIT:
            dk = self._new_sem(stream)
        self.cnt[dk] += 16
        self.ops[q].append((waits, fn, dk, 16))
        self._record(rsegs, wsegs, dk, self.cnt[dk])

    def coll(self, fn, reads=(), writes=()):
        deps, rsegs, wsegs = self._deps(reads, writes)
        if 'cc' not in self.cur:
            self._new_sem('cc')
        dk = self.cur['cc']
        if self.cnt[dk] > 0 and deps.get(dk, 0) < self.cnt[dk]:
            deps[dk] = self.cnt[dk]
        waits = []
        for sk, v in deps.items():
            if self.waited['pool'].get(sk, 0) < v:
                waits.append((sk, v))
                self.waited['pool'][sk] = v
        self.cnt[dk] += 1
        self.ops['pool'].append((waits, fn, dk, 1))
        self._record(rsegs, wsegs, dk, self.cnt[dk])

    def finish_wait(self, eng, accs):
        deps, _, _ = self._deps(accs, ())
        waits = []
        for sk, v in deps.items():
            if self.waited[eng].get(sk, 0) < v:
                waits.append((sk, v))
                self.waited[eng][sk] = v
        self.ops[eng].append((waits, None, None, 0))

    def emit(self):
        nc = self.nc
        with nc.Block() as block:
            def make(engname):
                def body(e):
                    for (waits, fn, sk, inc) in self.ops[engname]:
                        for (wk, v) in waits:
                            e.wait_ge(self.sem[wk], v)
                        if fn is not None:
                            fn(e).then_inc(self.sem[sk], inc)
                return body
            block.tensor(make('pe'))
            block.scalar(make('act'))
            block.vector(make('dve'))
            block.gpsimd(make('pool'))
            block.sync(make('sp'))


class Tl:
    def __init__(self, ap, res, lo, dims, es):
        self.ap = ap
        self.res = res
        self.lo = lo
        self.dims = list(dims)
        self.es = es
        n = es
        for d_ in dims:
            n *= d_
        self.nb = n

    def a(self, i=None, r=None):
        if i is None:
            return (self.res, self.lo, self.lo + self.nb)
        inner = self.es
        for d_ in self.dims[1:]:
            inner *= d_
        if isinstance(i, tuple):
            return (self.res, self.lo + i[0] * inner, self.lo + i[1] * inner)
        if r is None:
            return (self.res, self.lo + i * inner, self.lo + (i + 1) * inner)
        return (self.res, self.lo + i * inner + r[0] * self.es, self.lo + i * inner + r[1] * self.es)


def build(n_layers=2, pairs=None, dbg=(), stop_after=None):
    if pairs is None:
        pairs = [[0, 1], [2, 3], [4, 5], [6, 7]]
    nc = bass.Bass("TRN2", target_bir_lowering=False)

    def din(name, shape, dt=F32):
        return nc.dram_tensor(name, list(shape), dt, kind="ExternalInput").ap()

    xT_d = din("xT", [D, TOK])
    cT_d = din("cT", [128, 8])
    w_ada_d = din("w_ada_h", [2, D, 4608])
    b_adaT_d = din("b_adaT", [2, 128, 72])
    normsT_d = din("normsT", [128, 56])
    wf_in_d = [din("w_ffn1_in", [2, D, 2 * DFF]), din("w_ffn2_in", [2, D, 2 * DFF])]
    wf_out_d = [din("w_ffn1_out", [2, DFF, D]), din("w_ffn2_out", [2, DFF, D])]
    w_in_d = din("w_in", [2, D, 7168])
    w_glu_d = din("w_glu", [2, 512, 2048])
    w_ap_d = din("w_attn_proj", [2, 512, D])
    w_out_d = din("w_out", [2, D, D])
    biasg_d = din("biasg", [128, 24, 256])
    maskc_d = din("maskc", [128, 256])
    halom_d = din("halom", [128, 128])
    flagb_d = din("flagb", [128, 1])
    spc_d = din("sp_cols", [2, 128, 48])
    bTre_d = din("bT_re", [2, 128, 16, 128])
    bTim_d = din("bT_im", [2, 128, 16, 128])
    cpre_d = din("Cp_re", [2, 128, 16, 128])
    cpim_d = din("Cp_im", [2, 128, 16, 128])
    dmat_d = din("dmat", [2, 128, 4, 128])
    ident_d = din("ident", [128, 128])
    iota_d = din("iota", [128, 512])
    out_d = nc.dram_tensor("outT", [D, TOK], F32, kind="ExternalOutput").ap()
    dbg_d = {}
    for name in dbg:
        dbg_d[name] = nc.dram_tensor("dbg_" + name, [D, TOK], F32, kind="ExternalOutput").ap()

    xk_in = [[nc.dram_tensor("xk_in%d_%d" % (l, c), [128, 2 * KXC], BF16) for c in range(4)] for l in range(2)]
    xk_out = [[nc.dram_tensor("xk_out%d_%d" % (l, c), [256, 2 * KXC], BF16) for c in range(4)] for l in range(2)]
    sx_in = [nc.dram_tensor("sx_in%d" % l, [128, 32], F32) for l in range(2)]
    sx_out = [nc.dram_tensor("sx_out%d" % l, [256, 32], F32) for l in range(2)]
    md_in = nc.dram_tensor("md_in", [128, 72], F32)
    md_out = nc.dram_tensor("md_out", [256, 72], F32)

    with ExitStack() as es:
        S = Sched(nc, es)

        def sb(name, dims, dt):
            t = es.enter_context(nc.sbuf_tensor("sb_" + name, [128] + list(dims), dt))
            esz = 4 if dt == F32 else 2
            ap = t[:, :] if len(dims) == 1 else (t[:, :, :] if len(dims) == 2 else t[:, :, :, :])
            return Tl(ap, name, 0, dims, esz)

        XT = sb("XT", [KC, TOK], F32)
        HT = sb("HT", [KC, TOK], BF16)
        arena_t = es.enter_context(nc.sbuf_tensor("arena", [128, ARENA // 2], BF16))

        def ar(off, dims, dt):
            esz = 4 if dt == F32 else 2
            n = esz
            for d_ in dims:
                n *= d_
            assert off % 4 == 0 and off + n <= ARENA, (off, n)
            ap = arena_t[:, off // 2:(off + n) // 2]
            if dt == F32:
                ap = ap.bitcast(F32)
            if len(dims) == 2:
                ap = ap.rearrange("p (a b) -> p a b", a=dims[0])
            elif len(dims) == 3:
                ap = ap.rearrange("p (a b c) -> p a b c", a=dims[0], b=dims[1])
            return Tl(ap, "arena", off, dims, esz)

        PS = []
        for i in range(6):
            t = es.enter_context(nc.psum_tensor("ps%d" % i, [128, 512], F32))
            PS.append(Tl(t[:, :], "ps%d" % i, 0, [512], 4))
        PSB = []
        for i in range(2):
            t = es.enter_context(nc.psum_tensor("psb%d" % i, [128, 1024], BF16))
            PSB.append(Tl(t[:, :], "psb%d" % i, 0, [1024], 2))

        def ps_bf(i):
            return PS[i].ap.bitcast(BF16)

        ident_f = sb("ident_f", [128], F32)
        ident_b = sb("ident_b", [128], BF16)
        ones_b = sb("ones_b", [128], BF16)
        ones_f = sb("ones_f", [128], F32)
        iota_f = sb("iota_f", [512], F32)
        cT = sb("cT", [8], F32)
        cact = sb("cact", [8], BF16)
        modT = sb("modT", [2, 72], F32)
        badaT = sb("badaT", [2, 72], F32)
        normsT = sb("normsT", [56], F32)
        colA = sb("colA", [2, 3, 8], F32)
        colG = sb("colG", [2, 2, 8], F32)
        cst = sb("cst", [4], F32)
        maskc = sb("maskc", [256], F32)
        halom = sb("halom", [128], F32)
        scol = sb("scol", [24, 16], F32)
        ycol = sb("ycol", [32], F32)
        ytmp = sb("ytmp", [6, 32], F32)
        carr = sb("carr", [32], F32)
        cimp = sb("cimp", [32], F32)
        ctmp = sb("ctmp", [4], F32)
        psums = sb("psums", [4, 4, 16], F32)
        carr4 = sb("carr4", [4, 32], F32)
        ccol = sb("ccol", [10, 16], F32)

        def MM(o, oa, l, la, r, ra, st, sp_):
            S.op('pe', lambda e: e.matmul(o, lhsT=l, rhs=r, start=st, stop=sp_), reads=[la, ra], writes=[oa])

        def TR(o, oa, i, ia):
            S.op('pe', lambda e: e.transpose(o, i, ident_b.ap), reads=[ia, ident_b.a()], writes=[oa])

        def ACT(o, oa, i, ia, func, bias=None, scale=None, reads=()):
            kw = {}
            if bias is not None:
                kw['bias'] = bias
            if scale is not None:
                kw['scale'] = scale
            S.op('act', lambda e: e.activation(out=o, in_=i, func=func, **kw), reads=[ia] + list(reads), writes=[oa])

        def TT_(eng, o, oa, a, aa, b, ba, op):
            S.op(eng, lambda e: e.tensor_tensor(out=o, in0=a, in1=b, op=op), reads=[aa, ba], writes=[oa])

        def TS(eng, o, oa, a, aa, s1, s2, op0, op1=None, reads=()):
            if op1 is None:
                S.op(eng, lambda e: e.tensor_scalar(out=o, in0=a, scalar1=s1, scalar2=None, op0=op0),
                     reads=[aa] + list(reads), writes=[oa])
            else:
                S.op(eng, lambda e: e.tensor_scalar(out=o, in0=a, scalar1=s1, scalar2=s2, op0=op0, op1=op1),
                     reads=[aa] + list(reads), writes=[oa])

        def STT(o, oa, a, aa, sc, b, ba, op0, op1, reads=()):
            S.op('dve', lambda e: e.scalar_tensor_tensor(out=o, in0=a, scalar=sc, in1=b, op0=op0, op1=op1),
                 reads=[aa, ba] + list(reads), writes=[oa])

        def CP(eng, o, oa, i, ia):
            if eng == 'act':
                S.op('act', lambda e: e.copy(out=o, in_=i), reads=[ia], writes=[oa])
            else:
                S.op(eng, lambda e: e.tensor_copy(out=o, in_=i), reads=[ia], writes=[oa])

        def DMA(q, o, oa, i, ia):
            S.dma(q, lambda e: e.dma_start(out=o, in_=i), reads=[ia], writes=[oa])

        def dram(name):
            return (name, 0, INF)

        psrr = [0]

        def psum():
            i = psrr[0] % 6
            psrr[0] += 1
            return PS[i]

        DMA('sp', ident_f.ap, ident_f.a(), ident_d, dram("ident"))
        DMA('pool', ident_b.ap, ident_b.a(), ident_d, dram("ident"))
        DMA('sp', iota_f.ap, iota_f.a(), iota_d, dram("iota"))
        DMA('sp', cT.ap, cT.a(), cT_d, dram("cT"))
        DMA('sp', normsT.ap, normsT.a(), normsT_d, dram("normsT"))
        DMA('sp', badaT.ap, badaT.a(), b_adaT_d.rearrange("l p c -> p l c"), dram("b_adaT"))
        DMA('sp', maskc.ap, maskc.a(), maskc_d, dram("maskc"))
        DMA('sp', halom.ap, halom.a(), halom_d, dram("halom"))
        DMA('sp', cst.ap[:, 2:3], cst.a(), flagb_d, dram("flagb"))
        for m in range(KC):
            DMA('sp', XT.ap[:, m, :], XT.a(m), xT_d[m * 128:(m + 1) * 128, :], dram("xT"))
        S.op('pool', lambda e: e.memset(ones_b.ap, 1.0), writes=[ones_b.a()])
        S.op('pool', lambda e: e.memset(ones_f.ap, 1.0), writes=[ones_f.a()])
        S.op('pool', lambda e: e.memset(cst.ap[:, 0:1], 1e-6), writes=[cst.a()])
        S.op('pool', lambda e: e.memset(cst.ap[:, 1:2], math.pi / 2), writes=[cst.a()])
        S.op('pool', lambda e: e.memset(cst.ap[:, 3:4], 0.0), writes=[cst.a()])
        ACT(cact.ap, cact.a(), cT.ap, cT.a(), AF.Silu)

        wad = [ar(0, [KC, 512], BF16), ar(8192, [KC, 512], BF16)]
        madx = sb("madx", [72], F32)
        mall = sb("mall", [2, 72], F32)
        pm = psum()
        npc = 0
        for l in range(2):
            wv_ = w_ada_d[l].rearrange("(kc p) n -> p kc n", p=128)
            for pc in range(9):
                w = wad[npc % 2]
                npc += 1
                DMA('pool', w.ap, w.a(), wv_[:, :, pc * 512:(pc + 1) * 512], dram("w_ada_h"))
                for cc in range(4):
                    col = l * 36 + pc * 4 + cc
                    for k in range(KC):
                        MM(pm.ap[:, col:col + 1], pm.a(), w.ap[:, k, cc * 128:(cc + 1) * 128], w.a(),
                           cact.ap[:, k:k + 1], cact.a(), k == 0, k == KC - 1)
        CP('dve', madx.ap, madx.a(), pm.ap[:, 0:72], pm.a())
        DMA('sp', md_in.ap(), dram("md_in"), madx.ap, madx.a())
        S.coll(lambda e: e.collective_compute("AllGather", ALU.bypass, replica_groups=pairs,
                                              ins=[md_in.ap().opt()], outs=[md_out.ap().opt()]),
               reads=[dram("md_in")], writes=[dram("md_out")])
        DMA('sp', mall.ap[:, 0, :], mall.a(0), md_out.ap()[0:128, :], dram("md_out"))
        DMA('sp', mall.ap[:, 1, :], mall.a(1), md_out.ap()[128:256, :], dram("md_out"))
        for l in range(n_layers):
            for hf in range(2):
                TT_('dve', modT.ap[:, l, hf * 36:(hf + 1) * 36], modT.a(l), mall.ap[:, hf, l * 36:(l + 1) * 36], mall.a(hf),
                    badaT.ap[:, l, hf * 36:(hf + 1) * 36], badaT.a(l), ALU.add)
            for wi, sci in enumerate((1, 4, 7)):
                TS('dve', colA.ap[:, l, wi, :], colA.a(l), modT.ap[:, l, sci * 8:sci * 8 + 8], modT.a(l), 1.0, None, ALU.add)
                TT_('dve', colA.ap[:, l, wi, :], colA.a(l), colA.ap[:, l, wi, :], colA.a(l),
                    normsT.ap[:, l * 24 + wi * 8:l * 24 + wi * 8 + 8], normsT.a(), ALU.mult)
            for wi, gi in enumerate((2, 8)):
                TS('dve', colG.ap[:, l, wi, :], colG.a(l), modT.ap[:, l, gi * 8:gi * 8 + 8], modT.a(l), 0.5, None, ALU.mult)

        NT0 = 90112
        sq = [ar(NT0, [512], BF16), ar(NT0 + 1024, [512], BF16)]
        rs = ar(NT0 + 2048, [512], F32)
        ntm = [ar(NT0 + 4096, [512], F32), ar(NT0 + 6144, [512], F32)]

        def norm(acol, bcol, dst_bf):
            for tt in range(NTT):
                tsl = slice(tt * TT, (tt + 1) * TT)
                pss = psum()
                for m in range(KC):
                    s_ = sq[m % 2]
                    ACT(s_.ap, s_.a(), XT.ap[:, m, tsl], XT.a(m, (tt * TT, (tt + 1) * TT)), AF.Square)
                    MM(pss.ap, pss.a(), ones_b.ap, ones_b.a(), s_.ap, s_.a(), m == 0, m == KC - 1)
                ACT(rs.ap, rs.a(), pss.ap, pss.a(), AF.Sqrt, bias=cst.ap[:, 0:1], scale=1.0 / D, reads=[cst.a()])
                S.op('dve', lambda e: e.reciprocal(out=rs.ap, in_=rs.ap), reads=[rs.a()], writes=[rs.a()])
                for m in range(KC):
                    xa = XT.a(m, (tt * TT, (tt + 1) * TT))
                    a_ap, a_acc = acol(m)
                    if dst_bf:
                        t_ = ntm[m % 2]
                        STT(t_.ap, t_.a(), XT.ap[:, m, tsl], xa, a_ap, rs.ap, rs.a(), ALU.mult, ALU.mult, reads=[a_acc])
                        b_ap, b_acc = bcol(m)
                        if m % 2 == 0:
                            ACT(HT.ap[:, m, tsl], HT.a(m, (tt * TT, (tt + 1) * TT)), t_.ap, t_.a(), AF.Identity,
                                bias=b_ap, scale=1.0, reads=[b_acc])
                        else:
                            TS('dve', HT.ap[:, m, tsl], HT.a(m, (tt * TT, (tt + 1) * TT)), t_.ap, t_.a(), b_ap, None, ALU.add,
                               reads=[b_acc])
                    else:
                        STT(XT.ap[:, m, tsl], xa, XT.ap[:, m, tsl], xa, a_ap, rs.ap, rs.a(), ALU.mult, ALU.mult,
                            reads=[a_acc])

        def norm_mod(l, wi):
            shi = (0, 3, 6)[wi]
            norm(lambda m: (colA.ap[:, l, wi, m:m + 1], colA.a(l)),
                 lambda m: (modT.ap[:, l, shi * 8 + m:shi * 8 + m + 1], modT.a(l)), True)

        def ffn(l, which):
            win = wf_in_d[which][l].rearrange("(kc p) n -> p kc n", p=128)
            wout = wf_out_d[which][l].rearrange("(j p) n -> p j n", p=128)
            JH = FC // 2
            sT = ar(0, [JH, TOK], BF16)
            wa = [ar(45056 + i * 2048, [KC, 128], BF16) for i in range(2)]
            wb = [ar(49152 + i * 2048, [KC, 128], BF16) for i in range(2)]
            wo = [ar(53248 + i * 2816, [JH, 128], BF16) for i in range(2)]
            sg = [ar(58880 + i * 2048, [512], F32) for i in range(2)]
            wname = "w_ffn%d_in" % (which + 1)
            woname = "w_ffn%d_out" % (which + 1)
            for jh in range(2):
                def ld1(jj):
                    j = jh * JH + jj
                    DMA('pool', wa[jj % 2].ap, wa[jj % 2].a(), win[:, :, j * 128:(j + 1) * 128], dram(wname))
                    DMA('pool', wb[jj % 2].ap, wb[jj % 2].a(), win[:, :, DFF + j * 128:DFF + (j + 1) * 128], dram(wname))

                def ld2(m):
                    DMA('pool', wo[m % 2].ap, wo[m % 2].a(), wout[:, jh * JH:(jh + 1) * JH, m * 128:(m + 1) * 128], dram(woname))
                ld1(0)
                for jj in range(JH):
                    if jj + 1 < JH:
                        ld1(jj + 1)
                    else:
                        ld2(0)
                    for tt in range(NTT):
                        tsl = slice(tt * TT, (tt + 1) * TT)
                        pa = psum()
                        pb = psum()
                        for k in range(KC):
                            MM(pa.ap, pa.a(), wa[jj % 2].ap[:, k, :], wa[jj % 2].a(), HT.ap[:, k, tsl],
                               HT.a(k, (tt * TT, (tt + 1) * TT)), k == 0, k == KC - 1)
                        for k in range(KC):
                            MM(pb.ap, pb.a(), wb[jj % 2].ap[:, k, :], wb[jj % 2].a(), HT.ap[:, k, tsl],
                               HT.a(k, (tt * TT, (tt + 1) * TT)), k == 0, k == KC - 1)
                        g_ = sg[tt % 2]
                        ACT(g_.ap, g_.a(), pa.ap, pa.a(), AF.Silu)
                        TT_('dve', sT.ap[:, jj, tsl], sT.a(jj, (tt * TT, (tt + 1) * TT)), g_.ap, g_.a(), pb.ap, pb.a(), ALU.mult)
                for m in range(KC):
                    if m + 1 < KC:
                        ld2(m + 1)
                    for tt in range(NTT):
                        tsl = slice(tt * TT, (tt + 1) * TT)
                        po = psum()
                        for jj in range(JH):
                            MM(po.ap, po.a(), wo[m % 2].ap[:, jj, :], wo[m % 2].a(), sT.ap[:, jj, tsl],
                               sT.a(jj, (tt * TT, (tt + 1) * TT)), jj == 0, jj == JH - 1)
                        xa = XT.a(m, (tt * TT, (tt + 1) * TT))
                        STT(XT.ap[:, m, tsl], xa, po.ap, po.a(), colG.ap[:, l, which, m:m + 1], XT.ap[:, m, tsl], xa,
                            ALU.mult, ALU.add, reads=[colG.a(l)])

        def winv(l):
            return w_in_d[l].rearrange("(kc p) n -> p kc n", p=128)

        def blk_tokens(g, blk):
            d = GROUPS[g][1]
            nb = (TOK // d) // 128
            r, n = divmod(blk, nb)
            start = 128 * n * d + r
            return r, n, slice(start, start + 127 * d + 1, d) if d > 1 else slice(start, start + 128)

        def halo_export(l):
            R0 = 32768
            kexp = ar(R0, [4, KXC], BF16)
            vexp = ar(R0 + 21504, [4, KXC], BF16)
            wk = [ar(R0 + 43008 + i * 2048, [KC, 128], BF16) for i in range(2)]
            wv = [ar(R0 + 47104 + i * 2048, [KC, 128], BF16) for i in range(2)]
            wi = winv(l)
            it = 0
            for c in range(4):
                for g in range(3):
                    d = GROUPS[g][1]
                    kcol = 1536 + g * 512 + c * 128
                    vcol = 3072 + g * 512 + c * 128
                    wk_, wv_ = wk[it % 2], wv[it % 2]
                    it += 1
                    DMA('pool', wk_.ap, wk_.a(), wi[:, :, kcol:kcol + 128], dram("w_in"))
                    DMA('pool', wv_.ap, wv_.a(), wi[:, :, vcol:vcol + 128], dram("w_in"))
                    o0 = GOFFS[g]
                    if g == 0:
                        p = psum()
                        for k in range(KC):
                            MM(p.ap[:, 0:128], p.a(), wk_.ap[:, k, :], wk_.a(), HT.ap[:, k, 1920:2048], HT.a(k), k == 0, k == KC - 1)
                        CP('act', kexp.ap[:, c, o0:o0 + 128], kexp.a(c), p.ap[:, 0:128], p.a())
                    elif g == 1:
                        p = psum()
                        for k in range(KC):
                            MM(p.ap, p.a(), wk_.ap[:, k, :], wk_.a(), HT.ap[:, k, 1536:2048], HT.a(k), k == 0, k == KC - 1)
                        CP('act', kexp.ap[:, c, o0:o0 + 512].rearrange("p (r j) -> p r j", r=4), kexp.a(c),
                           p.ap.rearrange("p (j r) -> p r j", r=4), p.a())
                    else:
                        for tt in range(NTT):
                            p = psum()
                            for k in range(KC):
                                MM(p.ap, p.a(), wk_.ap[:, k, :], wk_.a(), HT.ap[:, k, tt * TT:(tt + 1) * TT], HT.a(k),
                                   k == 0, k == KC - 1)
                            CP('act' if tt % 2 == 0 else 'dve',
                               kexp.ap[:, c, o0:o0 + 2048].rearrange("p (r j) -> p r j", r=16)[:, :, 32 * tt:32 * tt + 32],
                               kexp.a(c), p.ap.rearrange("p (j r) -> p r j", r=16), p.a())
                    nb = (TOK // d) // 128
                    for r in range(d):
                        blk = r * nb + (nb - 1)
                        _, _, cols = blk_tokens(g, blk)
                        p = psum()
                        for k in range(KC):
                            MM(p.ap[:, 0:128], p.a(), HT.ap[:, k, cols], HT.a(k), wv_.ap[:, k, :], wv_.a(), k == 0, k == KC - 1)
                        CP('dve' if r % 2 == 0 else 'act', vexp.ap[:, c, o0 + r * 128:o0 + (r + 1) * 128], vexp.a(c),
                           p.ap[:, 0:128], p.a())
            for c in range(4):
                nm = "xk_in%d_%d" % (l, c)
                DMA('sp', xk_in[l][c].ap()[:, 0:KXC], dram(nm), kexp.ap[:, c, :], kexp.a(c))
                DMA('sp', xk_in[l][c].ap()[:, KXC:2 * KXC], dram(nm), vexp.ap[:, c, :], vexp.a(c))
                S.coll(lambda e, c=c: e.collective_compute("AllGather", ALU.bypass, replica_groups=pairs,
                                                           ins=[xk_in[l][c].ap().opt()], outs=[xk_out[l][c].ap().opt()]),
                       reads=[dram(nm)], writes=[dram("xk_out%d_%d" % (l, c))])

        UT0 = 32768
        SS0 = 49152

        def u_proj(l):
            uT = ar(UT0, [4, TOK], BF16)
            wu = [ar(SS0 + i * 2048, [KC, 128], BF16) for i in range(2)]
            wi = winv(l)
            for a in range(4):
                w = wu[a % 2]
                DMA('pool', w.ap, w.a(), wi[:, :, 4608 + a * 128:4608 + (a + 1) * 128], dram("w_in"))
                for tt in range(NTT):
                    p = psum()
                    for k in range(KC):
                        MM(p.ap, p.a(), w.ap[:, k, :], w.a(), HT.ap[:, k, tt * TT:(tt + 1) * TT], HT.a(k), k == 0, k == KC - 1)
                    CP('act', uT.ap[:, a, tt * TT:(tt + 1) * TT], uT.a(a, (tt * TT, (tt + 1) * TT)), p.ap, p.a())
            return uT

        SC = {n: i for i, n in enumerate(["lre", "lim", "ldt", "dt", "lrdt", "mag", "th", "cos1", "sin1", "c512", "s512",
                                           "are", "aim", "am1", "den", "t1", "t2", "fre", "fim", "t3"])}

        def sc(n):
            return scol.ap[:, SC[n], :]

        def ssm_cols(l):
            DMA('sp', scol.ap[:, 0:3, :], scol.a((0, 3)), spc_d[l].rearrange("p (a b) -> p a b", a=3), dram("sp_cols"))
            A_ = scol.a()

            def tt_(o, a, b, op):
                TT_('dve', sc(o), A_, sc(a), A_, sc(b), A_, op)
            ACT(sc("dt"), A_, sc("ldt"), A_, AF.Exp)
            tt_("lrdt", "lre", "dt", ALU.mult)
            ACT(sc("mag"), A_, sc("lrdt"), A_, AF.Exp)
            tt_("th", "lim", "dt", ALU.mult)
            Y = ycol.ap
            YA = ycol.a()
            TS('dve', Y[:, 0:16], YA, sc("th"), A_, 1.0 / TWO_PI, None, ALU.mult)
            TS('dve', Y[:, 16:32], YA, Y[:, 0:16], YA, 512.0, None, ALU.mult)
            T = ytmp.ap
            TA = ytmp.a()
            TS('dve', T[:, 0, :], TA, Y, YA, MAGIC, None, ALU.add)
            STT(T[:, 1, :], TA, T[:, 0, :], TA, -MAGIC, Y, YA, ALU.add, ALU.subtract)
            ACT(T[:, 2, :], TA, T[:, 1, :], TA, AF.Sin, scale=-TWO_PI)
            TS('dve', T[:, 3, :], TA, Y, YA, 0.25, MAGIC, ALU.add, ALU.add)
            STT(T[:, 4, :], TA, T[:, 3, :], TA, -MAGIC, Y, YA, ALU.add, ALU.subtract)
            ACT(T[:, 5, :], TA, T[:, 4, :], TA, AF.Sin, bias=cst.ap[:, 1:2], scale=-TWO_PI, reads=[cst.a()])
            CP('dve', sc("sin1"), A_, T[:, 2, 0:16], TA)
            CP('dve', sc("s512"), A_, T[:, 2, 16:32], TA)
            CP('dve', sc("cos1"), A_, T[:, 5, 0:16], TA)
            CP('dve', sc("c512"), A_, T[:, 5, 16:32], TA)
            tt_("are", "mag", "cos1", ALU.mult)
            tt_("aim", "mag", "sin1", ALU.mult)
            TS('dve', sc("am1"), A_, sc("are"), A_, -1.0, None, ALU.add)
            tt_("t1", "lre", "lre", ALU.mult)
            tt_("t2", "lim", "lim", ALU.mult)
            tt_("den", "t1", "t2", ALU.add)
            S.op('dve', lambda e: e.reciprocal(out=sc("den"), in_=sc("den")), reads=[A_], writes=[A_])
            tt_("t1", "am1", "lre", ALU.mult)
            tt_("t2", "aim", "lim", ALU.mult)
            tt_("t1", "t1", "t2", ALU.add)
            tt_("fre", "t1", "den", ALU.mult)
            tt_("t1", "aim", "lre", ALU.mult)
            tt_("t2", "am1", "lim", ALU.mult)
            tt_("t1", "t1", "t2", ALU.subtract)
            tt_("fim", "t1", "den", ALU.mult)

        def ssm_prep(l, second):
            bbr = ar(SS0, [16, 128], BF16)
            bbi = ar(SS0 + 4096, [16, 128], BF16)
            cpr = ar(SS0 + 8192, [16, 128], BF16)
            cpi = ar(SS0 + 12288, [16, 128], BF16)
            dm = ar(SS0 + 16384, [4, 128], BF16)
            T0 = SS0 + 17408
            bTr = ar(T0, [16, 128], F32)
            bTi = ar(T0 + 8192, [16, 128], F32)
            dg = [ar(T0 + 16384 + i * 512, [128], F32) for i in range(2)]
            pt = [ar(T0 + 17408 + i * 2048, [512], F32) for i in range(4)]
            DMA('sp', bTr.ap, bTr.a(), bTre_d[l], dram("bT_re"))
            DMA('sp', bTi.ap, bTi.a(), bTim_d[l], dram("bT_im"))
            if second:
                DMA('pool', cpr.ap, cpr.a(), cpre_d[l], dram("Cp_re"))
                DMA('pool', cpi.ap, cpi.a(), cpim_d[l], dram("Cp_im"))
                DMA('pool', dm.ap, dm.a(), dmat_d[l], dram("dmat"))
            A_ = scol.a()
            for a in range(4):
                pF = psum()
                pG = psum()
                for q in range(4):
                    i = 4 * a + q
                    for (pp, nm) in ((pF, "fre"), (pG, "fim")):
                        d_ = dg[0] if nm == "fre" else dg[1]
                        TS('dve', d_.ap, d_.a(), ident_f.ap, ident_f.a(), scol.ap[:, SC[nm], i:i + 1], None, ALU.mult, reads=[A_])
                        MM(pp.ap[:, q * 128:(q + 1) * 128], pp.a(), ones_f.ap, ones_f.a(), d_.ap, d_.a(), True, True)
                sl = slice(4 * a, 4 * a + 4)
                br = bTr.ap[:, sl, :].rearrange("p a b -> p (a b)")
                bi = bTi.ap[:, sl, :].rearrange("p a b -> p (a b)")
                bra = bTr.a((4 * a, 4 * a + 4))
                bia = bTi.a((4 * a, 4 * a + 4))
                TT_('dve', pt[0].ap, pt[0].a(), pF.ap, pF.a(), br, bra, ALU.mult)
                TT_('dve', pt[1].ap, pt[1].a(), pG.ap, pG.a(), bi, bia, ALU.mult)
                TT_('dve', bbr.ap[:, sl, :].rearrange("p a b -> p (a b)"), bbr.a((4 * a, 4 * a + 4)), pt[0].ap, pt[0].a(),
                    pt[1].ap, pt[1].a(), ALU.subtract)
                TT_('dve', pt[2].ap, pt[2].a(), pF.ap, pF.a(), bi, bia, ALU.mult)
                TT_('dve', pt[3].ap, pt[3].a(), pG.ap, pG.a(), br, bra, ALU.mult)
                TT_('dve', bbi.ap[:, sl, :].rearrange("p a b -> p (a b)"), bbi.a((4 * a, 4 * a + 4)), pt[2].ap, pt[2].a(),
                    pt[3].ap, pt[3].a(), ALU.add)
            return bbr, bbi, cpr, cpi, dm

        def ssm_pass(l, uT, mats, second, ygT=None):
            bbr, bbi, cpr, cpi, dm = mats
            usec4 = second and not os_.environ.get('SSM_P1_OLD')
            TB = SS0 + 17408
            ct = ar(TB, [512], F32)
            st = ar(TB + 2048, [512], F32)
            ty = ar(TB + 4096, [512], F32)
            tr = ar(TB + 6144, [512], F32)
            tn = ar(TB + 8192, [512], F32)
            tn2 = tn
            T0 = TB + 10240
            names = ["bur", "bui", "p1", "p2", "p3", "p4", "bmr", "bmi", "wr", "wi"]
            tm = {n: ar(T0 + i * 2048, [512], F32) for i, n in enumerate(names)}
            for qn, pn in (("q1", "p1"), ("q2", "p2"), ("q3", "p3"), ("q4", "p4")):
                tm[qn] = tm[pn]
            xr = ar(T0 + 20480, [512], BF16)
            nxi = ar(T0 + 21504, [512], BF16)
            A_ = scol.a()
            CA = carr.a()
            import os
            for i in range(int(os.environ.get('SSM_NT', '16'))):
                a, q = divmod(i, 4)
                TS('dve', ty.ap, ty.a(), iota_f.ap, iota_f.a(), ycol.ap[:, i:i + 1], None, ALU.mult, reads=[ycol.a()])
                TS('dve', tr.ap, tr.a(), ty.ap, ty.a(), MAGIC, None, ALU.add)
                STT(tn.ap, tn.a(), tr.ap, tr.a(), -MAGIC, ty.ap, ty.a(), ALU.add, ALU.subtract)
                ACT(st.ap, st.a(), tn.ap, tn.a(), AF.Sin, scale=-TWO_PI)
                TS('dve', tr.ap, tr.a(), ty.ap, ty.a(), 0.25, MAGIC, ALU.add, ALU.add)
                STT(tn2.ap, tn2.a(), tr.ap, tr.a(), -MAGIC, ty.ap, ty.a(), ALU.add, ALU.subtract)
                ACT(ct.ap, ct.a(), tn2.ap, tn2.a(), AF.Sin, bias=cst.ap[:, 1:2], scale=-TWO_PI, reads=[cst.a()])
                for tt in range(NTT):
                    tsl = slice(tt * TT, (tt + 1) * TT)
                    ua = uT.a(a, (tt * TT, (tt + 1) * TT))
                    par = 0 if second else 2 * ((4 * i + tt) % 2)
                    pbr = PS[par]
                    pbi = PS[par + 1]
                    MM(pbr.ap, pbr.a(), bbr.ap[:, i, :], bbr.a(i), uT.ap[:, a, tsl], ua, True, True)
                    MM(pbi.ap, pbi.a(), bbi.ap[:, i, :], bbi.a(i), uT.ap[:, a, tsl], ua, True, True)

                    def pm(o, x, t, pbr=pbr, pbi=pbi):
                        if x == "bur":
                            TT_('dve', tm[o].ap, tm[o].a(), pbr.ap, pbr.a(), t.ap, t.a(), ALU.mult)
                        elif x == "bui":
                            TT_('dve', tm[o].ap, tm[o].a(), pbi.ap, pbi.a(), t.ap, t.a(), ALU.mult)
                        else:
                            TT_('dve', tm[o].ap, tm[o].a(), tm[x].ap, tm[x].a(), t.ap, t.a(), ALU.mult)
                    pm("p1", "bur", ct)
                    pm("p2", "bui", st)
                    pm("p3", "bui", ct)
                    pm("p4", "bur", st)
                    TT_('dve', tm["bmr"].ap, tm["bmr"].a(), tm["p1"].ap, tm["p1"].a(), tm["p2"].ap, tm["p2"].a(), ALU.add)
                    TT_('dve', tm["bmi"].ap, tm["bmi"].a(), tm["p3"].ap, tm["p3"].a(), tm["p4"].ap, tm["p4"].a(), ALU.subtract)
                    magb = scol.ap[:, SC["mag"], i:i + 1].to_broadcast([128, TT])
                    for (w_, b_, cc) in (("wr", "bmr", i), ("wi", "bmi", 16 + i)):
                        wt, bt = tm[w_], tm[b_]
                        S.op('dve', lambda e, wt=wt, bt=bt, cc=cc, magb=magb, tt=tt: e.tensor_tensor_scan(
                            out=wt.ap, data0=magb, data1=bt.ap, initial=(carr4.ap[:, tt, cc:cc + 1] if usec4 else carr.ap[:, cc:cc + 1]), op0=ALU.mult, op1=ALU.add),
                            reads=[bt.a(), A_, CA, carr4.a()], writes=[wt.a()])
                    if not usec4:
                        wrl = tm["wr"].ap[:, TT - 1:TT]
                        wil = tm["wi"].ap[:, TT - 1:TT]
                        c5 = scol.ap[:, SC["c512"], i:i + 1]
                        s5 = scol.ap[:, SC["s512"], i:i + 1]
                        TT_('dve', ctmp.ap[:, 0:1], ctmp.a(), wil, tm["wi"].a(), s5, A_, ALU.mult)
                        TT_('dve', ctmp.ap[:, 1:2], ctmp.a(), wrl, tm["wr"].a(), s5, A_, ALU.mult)
                        STT(carr.ap[:, i:i + 1], CA, wrl, tm["wr"].a(), c5, ctmp.ap[:, 0:1], ctmp.a(), ALU.mult, ALU.subtract, reads=[A_])
                        STT(carr.ap[:, 16 + i:17 + i], CA, wil, tm["wi"].a(), c5, ctmp.ap[:, 1:2], ctmp.a(), ALU.mult, ALU.add, reads=[A_])
                    if second:
                        qb = [ar(T0 + (2 + k_) * 2048, [512], BF16) for k_ in range(4)]
                        TT_('dve', qb[0].ap, qb[0].a(), tm["wr"].ap, tm["wr"].a(), ct.ap, ct.a(), ALU.mult)
                        STT(qb[1].ap, qb[1].a(), tm["wi"].ap, tm["wi"].a(), -1.0, st.ap, st.a(), ALU.mult, ALU.mult)
                        STT(qb[2].ap, qb[2].a(), tm["wi"].ap, tm["wi"].a(), -1.0, ct.ap, ct.a(), ALU.mult, ALU.mult)
                        STT(qb[3].ap, qb[3].a(), tm["wr"].ap, tm["wr"].a(), -1.0, st.ap, st.a(), ALU.mult, ALU.mult)
                        py = PS[2 + tt]
                        if q == 0:
                            MM(py.ap, py.a(), dm.ap[:, a, :], dm.a(a), uT.ap[:, a, tsl], ua, True, False)
                        MM(py.ap, py.a(), cpr.ap[:, i, :], cpr.a(i), qb[0].ap, qb[0].a(), False, False)
                        MM(py.ap, py.a(), cpr.ap[:, i, :], cpr.a(i), qb[1].ap, qb[1].a(), False, False)
                        MM(py.ap, py.a(), cpi.ap[:, i, :], cpi.a(i), qb[2].ap, qb[2].a(), False, False)
                        MM(py.ap, py.a(), cpi.ap[:, i, :], cpi.a(i), qb[3].ap, qb[3].a(), False, q == 3)
                        if q == 3:
                            ACT(ygT.ap[:, a, tsl], ygT.a(a, (tt * TT, (tt + 1) * TT)), py.ap, py.a(), AF.Gelu_apprx_tanh)

        def ssm_pass1_fast(l, uT, mats):
            bbr, bbi, cpr, cpi, dm = mats
            TB = SS0 + 17408
            ct = ar(TB, [512], F32)
            st = ar(TB + 2048, [512], F32)
            ty = ar(TB + 4096, [512], F32)
            tr = ar(TB + 6144, [512], F32)
            tn = ar(TB + 8192, [512], F32)
            T0 = TB + 10240
            junk = ar(T0, [512], F32)
            wre = ar(T0 + 2048, [512], F32)
            wim = ar(T0 + 4096, [512], F32)
            mp = ar(T0 + 6144, [512], F32)
            rio = ar(T0 + 8192, [512], F32)
            A_ = scol.a()
            TS('dve', rio.ap, rio.a(), iota_f.ap, iota_f.a(), -1.0, 511.0, ALU.mult, ALU.add)
            for i in range(16):
                a, q = divmod(i, 4)
                TS('dve', ty.ap, ty.a(), iota_f.ap, iota_f.a(), ycol.ap[:, i:i + 1], None, ALU.mult, reads=[ycol.a()])
                TS('dve', tr.ap, tr.a(), ty.ap, ty.a(), MAGIC, None, ALU.add)
                STT(tn.ap, tn.a(), tr.ap, tr.a(), -MAGIC, ty.ap, ty.a(), ALU.add, ALU.subtract)
                ACT(st.ap, st.a(), tn.ap, tn.a(), AF.Sin, scale=-TWO_PI)
                TS('dve', tr.ap, tr.a(), ty.ap, ty.a(), 0.25, MAGIC, ALU.add, ALU.add)
                STT(tn.ap, tn.a(), tr.ap, tr.a(), -MAGIC, ty.ap, ty.a(), ALU.add, ALU.subtract)
                ACT(ct.ap, ct.a(), tn.ap, tn.a(), AF.Sin, bias=cst.ap[:, 1:2], scale=-TWO_PI, reads=[cst.a()])
                ACT(mp.ap, mp.a(), rio.ap, rio.a(), AF.Exp, scale=scol.ap[:, SC["lrdt"], i:i + 1], reads=[A_])
                TT_('dve', wre.ap, wre.a(), mp.ap, mp.a(), ct.ap, ct.a(), ALU.mult)
                TT_('dve', wim.ap, wim.a(), mp.ap, mp.a(), st.ap, st.a(), ALU.mult)
                for tt in range(NTT):
                    tsl = slice(tt * TT, (tt + 1) * TT)
                    ua = uT.a(a, (tt * TT, (tt + 1) * TT))
                    par = 2 * ((4 * i + tt) % 2)
                    pbr = PS[par]
                    pbi = PS[par + 1]
                    MM(pbr.ap, pbr.a(), bbr.ap[:, i, :], bbr.a(i), uT.ap[:, a, tsl], ua, True, True)
                    MM(pbi.ap, pbi.a(), bbi.ap[:, i, :], bbi.a(i), uT.ap[:, a, tsl], ua, True, True)
                    for k_, (pp, ww) in enumerate(((pbr, wre), (pbi, wim), (pbi, wre), (pbr, wim))):
                        S.op('dve', lambda e, pp=pp, ww=ww, k_=k_, tt=tt, i=i: e.scalar_tensor_tensor(
                            out=junk.ap, in0=pp.ap, scalar=1.0, in1=ww.ap, op0=ALU.mult, op1=ALU.mult,
                            accum_out=psums.ap[:, k_, tt, i:i + 1]),
                            reads=[pp.a(), ww.a()], writes=[junk.a(), psums.a()])
            CC = ccol.a()

            def cc(n):
                return ccol.ap[:, n, :]
            ACT(cc(0), CC, scol.ap[:, SC["lrdt"], :], A_, AF.Exp, scale=512.0)
            S.op('dve', lambda e: e.memset(ccol.ap[:, 1:3, :], 0.0), writes=[CC])
            c5 = scol.ap[:, SC["c512"], :]
            s5 = scol.ap[:, SC["s512"], :]
            PA = psums.a()
            for tt in range(NTT):
                TT_('dve', cc(3), CC, psums.ap[:, 0, tt, :], PA, psums.ap[:, 1, tt, :], PA, ALU.add)
                TT_('dve', cc(4), CC, psums.ap[:, 2, tt, :], PA, psums.ap[:, 3, tt, :], PA, ALU.subtract)
                TT_('dve', cc(5), CC, cc(1), CC, cc(0), CC, ALU.mult)
                TT_('dve', cc(5), CC, cc(5), CC, cc(3), CC, ALU.add)
                TT_('dve', cc(6), CC, cc(2), CC, cc(0), CC, ALU.mult)
                TT_('dve', cc(6), CC, cc(6), CC, cc(4), CC, ALU.add)
                TT_('dve', cc(7), CC, cc(5), CC, c5, A_, ALU.mult)
                TT_('dve', cc(8), CC, cc(6), CC, s5, A_, ALU.mult)
                TT_('dve', cc(1), CC, cc(7), CC, cc(8), CC, ALU.subtract)
                TT_('dve', cc(7), CC, cc(6), CC, c5, A_, ALU.mult)
                TT_('dve', cc(8), CC, cc(5), CC, s5, A_, ALU.mult)
                TT_('dve', cc(2), CC, cc(7), CC, cc(8), CC, ALU.add)
            CP('dve', carr.ap[:, 0:16], carr.a(), cc(1), CC)
            CP('dve', carr.ap[:, 16:32], carr.a(), cc(2), CC)

        def carry_exchange(l):
            nm = "sx_in%d" % l
            DMA('sp', sx_in[l].ap(), dram(nm), carr.ap, carr.a())
            S.coll(lambda e: e.collective_compute("AllGather", ALU.bypass, replica_groups=pairs,
                                                  ins=[sx_in[l].ap().opt()], outs=[sx_out[l].ap().opt()]),
                   reads=[dram(nm)], writes=[dram("sx_out%d" % l)])
            DMA('sp', cimp.ap, cimp.a(), sx_out[l].ap()[0:128, :], dram("sx_out%d" % l))
            TS('dve', carr.ap, carr.a(), cimp.ap, cimp.a(), cst.ap[:, 2:3], None, ALU.mult, reads=[cst.a()])

        def carry_chain(l):
            CC = ccol.a()
            A_ = scol.a()
            PA = psums.a()
            C4 = carr4.a()

            def cc(n):
                return ccol.ap[:, n, :]
            c5 = scol.ap[:, SC["c512"], :]
            s5 = scol.ap[:, SC["s512"], :]
            CP('dve', cc(1), CC, carr.ap[:, 0:16], carr.a())
            CP('dve', cc(2), CC, carr.ap[:, 16:32], carr.a())
            for tt in range(NTT):
                CP('dve', carr4.ap[:, tt, 0:16], C4, cc(1), CC)
                CP('dve', carr4.ap[:, tt, 16:32], C4, cc(2), CC)
                if tt == NTT - 1:
                    break
                TT_('dve', cc(3), CC, psums.ap[:, 0, tt, :], PA, psums.ap[:, 1, tt, :], PA, ALU.add)
                TT_('dve', cc(4), CC, psums.ap[:, 2, tt, :], PA, psums.ap[:, 3, tt, :], PA, ALU.subtract)
                TT_('dve', cc(5), CC, cc(1), CC, cc(0), CC, ALU.mult)
                TT_('dve', cc(5), CC, cc(5), CC, cc(3), CC, ALU.add)
                TT_('dve', cc(6), CC, cc(2), CC, cc(0), CC, ALU.mult)
                TT_('dve', cc(6), CC, cc(6), CC, cc(4), CC, ALU.add)
                TT_('dve', cc(7), CC, cc(5), CC, c5, A_, ALU.mult)
                TT_('dve', cc(8), CC, cc(6), CC, s5, A_, ALU.mult)
                TT_('dve', cc(1), CC, cc(7), CC, cc(8), CC, ALU.subtract)
                TT_('dve', cc(7), CC, cc(6), CC, c5, A_, ALU.mult)
                TT_('dve', cc(8), CC, cc(5), CC, s5, A_, ALU.mult)
                TT_('dve', cc(2), CC, cc(7), CC, cc(8), CC, ALU.add)

        def attention(l):
            oT = ar(0, [4, TOK], BF16)
            R0 = 32768
            wq = [ar(R0 + i * 2048, [KC, 128], BF16) for i in range(2)]
            wk = [ar(R0 + 4096 + i * 2048, [KC, 128], BF16) for i in range(2)]
            wv = [ar(R0 + 8192 + i * 2048, [KC, 128], BF16) for i in range(2)]
            qT = ar(R0 + 12288, [TOK], BF16)
            kT = ar(R0 + 16384, [TOK], BF16)
            vB = ar(R0 + 20480, [16, 128], BF16)
            kh = ar(R0 + 24576, [TOK], BF16)
            vh = ar(R0 + 28672, [16, 128], BF16)
            Oacc = ar(R0 + 32768, [TOK], F32)
            Dacc = ar(R0 + 40960, [TOK], F32)
            bN = ar(R0 + 49152, [6, 256], F32)
            bF = ar(R0 + 55296, [6, 256], F32)
            Sf = [ar(R0 + 61440 + i * 1024, [256], F32) for i in range(3)]
            Pb = [ar(R0 + 64512 + i * 512, [256], BF16) for i in range(3)]
            PTs = [ar(R0 + 66048 + i * 512, [2, 128], BF16) for i in range(3)]
            wi = winv(l)
            it = 0
            uc = 0
            import os
            AST = int(os.environ.get('ATT_STAGE', '6'))
            for c in range(int(os.environ.get('ATT_C', '4'))):
                for g in range(3):
                    h0 = g * 8 + 2 * c
                    DMA('sp', bN.ap[:, 2 * g:2 * g + 2, :], bN.a((2 * g, 2 * g + 2)), biasg_d[:, h0:h0 + 2, :], dram("biasg"))
                for j in range(6):
                    TT_('dve', bN.ap[:, j, :], bN.a(j), bN.ap[:, j, :], bN.a(j), maskc.ap, maskc.a(), ALU.add)
                    CP('dve', bF.ap[:, j, 128:256], bF.a(j), bN.ap[:, j, 128:256], bN.a(j))
                    TT_('dve', bF.ap[:, j, 0:128], bF.a(j), bN.ap[:, j, 0:128], bN.a(j), halom.ap, halom.a(), ALU.add)
                S.op('pool', lambda e: e.memset(Oacc.ap, 0.0), writes=[Oacc.a()])
                S.op('pool', lambda e: e.memset(Dacc.ap, 0.0), writes=[Dacc.a()])
                for g in range(int(os.environ.get('ATT_G', '3'))):
                    d = GROUPS[g][1]
                    Lc = TOK // d
                    nb = Lc // 128
                    qcol = g * 512 + c * 128
                    wq_, wk_, wv_ = wq[it % 2], wk[it % 2], wv[it % 2]
                    it += 1
                    DMA('pool', wq_.ap, wq_.a(), wi[:, :, qcol:qcol + 128], dram("w_in"))
                    DMA('pool', wk_.ap, wk_.a(), wi[:, :, 1536 + qcol:1536 + qcol + 128], dram("w_in"))
                    DMA('pool', wv_.ap, wv_.a(), wi[:, :, 3072 + qcol:3072 + qcol + 128], dram("w_in"))
                    o0 = GOFFS[g]
                    xo = xk_out[l][c].ap()
                    xon = dram("xk_out%d_%d" % (l, c))
                    DMA('sp', kh.ap[:, 0:d * 128], kh.a(), xo[0:128, o0:o0 + d * 128], xon)
                    DMA('sp', vh.ap[:, 0:d, :].rearrange("p a b -> p (a b)"), vh.a(),
                        xo[0:128, KXC + o0:KXC + o0 + d * 128], xon)
                    for (w_, dst, eng) in (((wq_, qT, 'act'), (wk_, kT, 'dve')) if AST >= 2 else ()):
                        for tt in range(NTT):
                            p = psum()
                            for k in range(KC):
                                MM(p.ap, p.a(), w_.ap[:, k, :], w_.a(), HT.ap[:, k, tt * TT:(tt + 1) * TT], HT.a(k),
                                   k == 0, k == KC - 1)
                            if d == 1:
                                o_ap = dst.ap[:, tt * TT:(tt + 1) * TT]
                                i_ap = p.ap
                            else:
                                jn = TT // d
                                o_ap = dst.ap.rearrange("p (r j) -> p r j", r=d)[:, :, jn * tt:jn * (tt + 1)]
                                i_ap = p.ap.rearrange("p (j r) -> p r j", r=d)
                            CP(eng, o_ap, dst.a(), i_ap, p.a())
                    for blk in (range(16) if AST >= 3 else ()):
                        _, _, cols = blk_tokens(g, blk)
                        p = psum()
                        for k in range(KC):
                            MM(p.ap[:, 0:128], p.a(), HT.ap[:, k, cols], HT.a(k), wv_.ap[:, k, :], wv_.a(), k == 0, k == KC - 1)
                        CP('act' if blk % 2 == 0 else 'dve', vB.ap[:, blk, :], vB.a(blk), p.ap[:, 0:128], p.a())
                    units = [(blk, hh) for blk in range(16) for hh in range(2)]
                    NU = len(units)

                    def uinfo(k):
                        blk, hh = units[k]
                        r, n, cols = blk_tokens(g, blk)
                        return blk, hh, r, n, cols, (n == 0), r * Lc + 128 * n, slice(64 * hh, 64 * hh + 64)

                    def stA(k):
                        blk, hh, r, n, cols, first, q0, pr = uinfo(k)
                        pS = PS[4 + k % 2]
                        if first:
                            kprev, kpa = kh.ap[pr, r * 128:(r + 1) * 128], kh.a()
                        else:
                            kprev, kpa = kT.ap[pr, q0 - 128:q0], kT.a()
                        if first:
                            MM(pS.ap[:, 0:128], pS.a(), qT.ap[pr, q0:q0 + 128], qT.a(), kprev, kpa, True, True)
                            MM(pS.ap[:, 128:256], pS.a(), qT.ap[pr, q0:q0 + 128], qT.a(), kT.ap[pr, q0:q0 + 128], kT.a(), True, True)
                        else:
                            MM(pS.ap[:, 0:256], pS.a(), qT.ap[pr, q0:q0 + 128], qT.a(), kT.ap[pr, q0 - 128:q0 + 128], kT.a(), True, True)
                        bt = bF if first else bN
                        u3 = k % 3
                        STT(Sf[u3].ap, Sf[u3].a(), pS.ap[:, 0:256], pS.a(), 0.125, bt.ap[:, 2 * g + hh, :], bt.a(2 * g + hh), ALU.mult, ALU.add)
                        ACT(Pb[u3].ap, Pb[u3].a(), Sf[u3].ap, Sf[u3].a(), AF.Exp)

                    def stB(k):
                        u3 = k % 3
                        pT = PSB[k % 2]
                        TR(pT.ap[:, 0:128], pT.a(), Pb[u3].ap[:, 0:128], Pb[u3].a())
                        TR(pT.ap[:, 128:256], pT.a(), Pb[u3].ap[:, 128:256], Pb[u3].a())
                        CP('act', PTs[u3].ap.rearrange("p a b -> p (a b)"), PTs[u3].a(), pT.ap[:, 0:256], pT.a())

                    def stC(k):
                        blk, hh, r, n, cols, first, q0, pr = uinfo(k)
                        u3 = k % 3
                        po = PS[2 * (blk % 2)]
                        pd = PS[2 * (blk % 2) + 1]
                        if first:
                            vprev, vpa = vh.ap[:, r, 64 * hh:64 * hh + 64], vh.a()
                        else:
                            vprev, vpa = vB.ap[:, blk - 1, 64 * hh:64 * hh + 64], vB.a(blk - 1)
                        MM(po.ap[pr, 0:128], po.a(), vprev, vpa, PTs[u3].ap[:, 0, :], PTs[u3].a(), True, False)
                        MM(po.ap[pr, 0:128], po.a(), vB.ap[:, blk, 64 * hh:64 * hh + 64], vB.a(blk), PTs[u3].ap[:, 1, :],
                           PTs[u3].a(), False, True)
                        MM(pd.ap[pr, 0:128], pd.a(), ones_b.ap[:, 0:64], ones_b.a(), PTs[u3].ap[:, 0, :], PTs[u3].a(), True, False)
                        MM(pd.ap[pr, 0:128], pd.a(), ones_b.ap[:, 0:64], ones_b.a(), PTs[u3].ap[:, 1, :], PTs[u3].a(), False, True)
                        if hh == 1:
                            STT(Oacc.ap[:, cols], Oacc.a(), po.ap[:, 0:128], po.a(), 1.0, Oacc.ap[:, cols], Oacc.a(), ALU.mult, ALU.add)
                            STT(Dacc.ap[:, cols], Dacc.a(), pd.ap[:, 0:128], pd.a(), 1.0, Dacc.ap[:, cols], Dacc.a(), ALU.mult, ALU.add)

                    for k in range(NU + 2):
                        if k < NU:
                            stA(k)
                        if 1 <= k <= NU:
                            stB(k - 1)
                        if 2 <= k <= NU + 1:
                            stC(k - 2)
                S.op('dve', lambda e: e.reciprocal(out=Dacc.ap, in_=Dacc.ap), reads=[Dacc.a()], writes=[Dacc.a()])
                TT_('dve', oT.ap[:, c, :], oT.a(c), Oacc.ap, Oacc.a(), Dacc.ap, Dacc.a(), ALU.mult)
            return oT

        def mix_tail(l, oT, ygT):
            R0 = 32768
            mix = ar(R0, [KC, TOK], BF16)
            wgl = [ar(R0 + 32768 + i * 2048, [4, 256], BF16) for i in range(2)]
            wgt = [ar(R0 + 36864 + i * 4096, [KC, 256], BF16) for i in range(2)]
            wap = [ar(R0 + 45056 + i * 1024, [4, 128], BF16) for i in range(2)]
            wou = [ar(R0 + 47104 + i * 2048, [KC, 128], BF16) for i in range(2)]
            tmp = [ar(R0 + 51200 + i * 2048, [512], F32) for i in range(5)]
            wi = winv(l)
            wg_d = w_glu_d[l].rearrange("(kc p) n -> p kc n", p=128)
            wa_d = w_ap_d[l].rearrange("(kc p) n -> p kc n", p=128)
            wo_d = w_out_d[l].rearrange("(kc p) n -> p kc n", p=128)
            for m in range(KC):
                g_, t_, a_ = wgl[m % 2], wgt[m % 2], wap[m % 2]
                DMA('pool', g_.ap[:, :, 0:128], g_.a(), wg_d[:, :, m * 128:(m + 1) * 128], dram("w_glu"))
                DMA('pool', g_.ap[:, :, 128:256], g_.a(), wg_d[:, :, 1024 + m * 128:1024 + (m + 1) * 128], dram("w_glu"))
                DMA('pool', t_.ap[:, :, 0:128], t_.a(), wi[:, :, 5120 + m * 128:5120 + (m + 1) * 128], dram("w_in"))
                DMA('pool', t_.ap[:, :, 128:256], t_.a(), wi[:, :, 6144 + m * 128:6144 + (m + 1) * 128], dram("w_in"))
                DMA('pool', a_.ap, a_.a(), wa_d[:, :, m * 128:(m + 1) * 128], dram("w_attn_proj"))
                for tt in range(NTT):
                    tsl = slice(tt * TT, (tt + 1) * TT)
                    pga, pgb, pgs, pgt, pya = psum(), psum(), psum(), psum(), psum()
                    for k in range(4):
                        MM(pga.ap, pga.a(), g_.ap[:, k, 0:128], g_.a(), ygT.ap[:, k, tsl], ygT.a(k), k == 0, k == 3)
                    for k in range(4):
                        MM(pgb.ap, pgb.a(), g_.ap[:, k, 128:256], g_.a(), ygT.ap[:, k, tsl], ygT.a(k), k == 0, k == 3)
                    for k in range(KC):
                        MM(pgt.ap, pgt.a(), t_.ap[:, k, 0:128], t_.a(), HT.ap[:, k, tsl], HT.a(k), k == 0, k == KC - 1)
                    for k in range(KC):
                        MM(pgs.ap, pgs.a(), t_.ap[:, k, 128:256], t_.a(), HT.ap[:, k, tsl], HT.a(k), k == 0, k == KC - 1)
                    for k in range(4):
                        MM(pya.ap, pya.a(), a_.ap[:, k, :], a_.a(), oT.ap[:, k, tsl], oT.a(k), k == 0, k == 3)
                    ACT(tmp[0].ap, tmp[0].a(), pgb.ap, pgb.a(), AF.Sigmoid)
                    ACT(tmp[1].ap, tmp[1].a(), pgs.ap, pgs.a(), AF.Sigmoid)
                    ACT(tmp[2].ap, tmp[2].a(), pgt.ap, pgt.a(), AF.Sigmoid)
                    TT_('dve', tmp[3].ap, tmp[3].a(), pga.ap, pga.a(), tmp[0].ap, tmp[0].a(), ALU.mult)
                    TT_('dve', tmp[3].ap, tmp[3].a(), tmp[3].ap, tmp[3].a(), tmp[1].ap, tmp[1].a(), ALU.mult)
                    TT_('dve', tmp[4].ap, tmp[4].a(), pya.ap, pya.a(), tmp[2].ap, tmp[2].a(), ALU.mult)
                    TT_('dve', mix.ap[:, m, tsl], mix.a(m, (tt * TT, (tt + 1) * TT)), tmp[3].ap, tmp[3].a(), tmp[4].ap, tmp[4].a(), ALU.add)
            for m in range(KC):
                w_ = wou[m % 2]
                DMA('pool', w_.ap, w_.a(), wo_d[:, :, m * 128:(m + 1) * 128], dram("w_out"))
                for tt in range(NTT):
                    tsl = slice(tt * TT, (tt + 1) * TT)
                    p = psum()
                    for k in range(KC):
                        MM(p.ap, p.a(), w_.ap[:, k, :], w_.a(), mix.ap[:, k, tsl], mix.a(k, (tt * TT, (tt + 1) * TT)), k == 0, k == KC - 1)
                    xa = XT.a(m, (tt * TT, (tt + 1) * TT))
                    STT(XT.ap[:, m, tsl], xa, p.ap, p.a(), modT.ap[:, l, 40 + m:41 + m], XT.ap[:, m, tsl], xa, ALU.mult, ALU.add,
                        reads=[modT.a(l)])

        def dump(name):
            if name in dbg_d:
                for m in range(KC):
                    DMA('sp', dbg_d[name][m * 128:(m + 1) * 128, :], dram("dbg_" + name), XT.ap[:, m, :], XT.a(m))

        def dump_bf(name, tl, nchunk):
            if name in dbg_d:
                tmpf = ar(81920, [TOK], F32)
                for m in range(nchunk):
                    CP('dve', tmpf.ap, tmpf.a(), tl.ap[:, m, :], tl.a(m))
                    DMA('sp', dbg_d[name][m * 128:(m + 1) * 128, :], dram("dbg_" + name), tmpf.ap, tmpf.a())

        done = False
        for l in range(n_layers):
            norm_mod(l, 0)
            ffn(l, 0)
            dump("x1_%d" % l)
            if stop_after == "ffn1":
                done = True
                break
            norm_mod(l, 1)
            halo_export(l)
            if stop_after == "halo":
                done = True
                break
            S.op('pool', lambda e: e.memset(carr.ap, 0.0), writes=[carr.a()])
            ssm_cols(l)
            uT = u_proj(l)
            mats = ssm_prep(l, False)
            if stop_after == "prep":
                done = True
                break
            if os_.environ.get('SSM_P1_OLD'):
                ssm_pass(l, uT, mats, False)
            else:
                ssm_pass1_fast(l, uT, mats)
            carry_exchange(l)
            fastc = not os_.environ.get('SSM_P1_OLD')
            if fastc:
                carry_chain(l)
            if stop_after == "ssm1":
                done = True
                break
            oT = attention(l)
            dump_bf("oT_%d" % l, oT, 4)
            if stop_after == "attn":
                done = True
                break
            uT = u_proj(l)
            mats = ssm_prep(l, True)
            ygT = ar(16384, [4, TOK], BF16)
            ssm_pass(l, uT, mats, True, ygT)
            if stop_after == "ssm2":
                done = True
                break
            dump_bf("yg_%d" % l, ygT, 4)
            mix_tail(l, oT, ygT)
            dump("x2_%d" % l)
            if stop_after == "mix":
                done = True
                break
            norm_mod(l, 2)
            ffn(l, 1)
            dump("x3_%d" % l)
        if not done:
            norm(lambda m: (normsT.ap[:, 48 + m:49 + m], normsT.a()), None, False)
        fin = [dram("outT")]
        for m in range(KC):
            DMA('sp', out_d[m * 128:(m + 1) * 128, :], dram("outT"), XT.ap[:, m, :], XT.a(m))
        for name in dbg_d:
            fin.append(dram("dbg_" + name))
        S.finish_wait('sp', fin)
        S.emit()
    return nc


def _t5_bucket(dist):
    max_exact = 16
    dd = np.maximum(dist, max_exact).astype(np.float32)
    large = max_exact + (np.log(dd / max_exact) / np.log(2048 / max_exact) * (32 - max_exact)).astype(np.int32)
    large = np.minimum(large, 31)
    return np.where(dist < max_exact, dist, large).astype(np.int32)


def prep_shared(inp):
    f = np.float32
    sh = {}
    for k in ("w_ffn1_in", "w_ffn1_out", "w_ffn2_in", "w_ffn2_out", "w_in", "w_glu", "w_attn_proj", "w_out"):
        sh[k] = np.ascontiguousarray(inp[k], dtype=f)
    sh["b_adaT"] = np.ascontiguousarray(inp["b_ada"].reshape(2, 72, 128).transpose(0, 2, 1), dtype=f)
    norms = np.zeros((128, 56), f)
    for l in range(2):
        for wi, nm in enumerate(("norm_ffn1", "norm_mix", "norm_ffn2")):
            norms[:, l * 24 + wi * 8:l * 24 + wi * 8 + 8] = inp[nm][l].reshape(8, 128).T
    norms[:, 48:56] = inp["final_norm"].reshape(8, 128).T
    sh["normsT"] = norms
    qi = np.arange(128)[:, None]
    kj = np.arange(256)[None, :]
    rel = 128 + qi - kj
    band = (rel >= 0) & (rel <= 128)
    biasg = np.zeros((128, 24, 256), f)
    for g, (window, dil) in enumerate(GROUPS):
        bucket = _t5_bucket(np.clip(rel, 0, None) * dil)
        for h in range(8):
            biasg[:, g * 8 + h, :] = inp["rel_bias"][bucket, g * 8 + h]
    sh["biasg"] = biasg
    sh["maskc"] = np.where(band, 0.0, NEG).astype(f)
    spc = np.zeros((2, 128, 48), f)
    bTr = np.zeros((2, 128, 16, 128), f)
    bTi = np.zeros((2, 128, 16, 128), f)
    cpr = np.zeros((2, 128, 16, 128), f)
    cpi = np.zeros((2, 128, 16, 128), f)
    dm = np.zeros((2, 128, 4, 128), f)
    for l in range(2):
        for i in range(16):
            for gl in range(2):
                g = 2 * i + gl
                ps_ = slice(gl * 64, gl * 64 + 64)
                spc[l, ps_, i] = inp["lam_re"][l, g]
                spc[l, ps_, 16 + i] = inp["lam_im"][l, g]
                spc[l, ps_, 32 + i] = inp["log_dt"][l, g]
                ch0 = (i % 4) * 32 + gl * 16
                bTr[l, ch0:ch0 + 16, i, ps_] = inp["b_re"][l, g].T
                bTi[l, ch0:ch0 + 16, i, ps_] = inp["b_im"][l, g].T
                cpr[l, ps_, i, ch0:ch0 + 16] = inp["c_re"][l, g].T
                cpi[l, ps_, i, ch0:ch0 + 16] = inp["c_im"][l, g].T
        dflat = inp["d_skip"][l].reshape(512)
        for a in range(4):
            dm[l, np.arange(128), a, np.arange(128)] = dflat[a * 128:(a + 1) * 128]
    sh["sp_cols"] = spc
    sh["bT_re"] = bTr
    sh["bT_im"] = bTi
    sh["Cp_re"] = cpr
    sh["Cp_im"] = cpi
    sh["dmat"] = dm
    sh["ident"] = np.eye(128, dtype=f)
    sh["iota"] = np.ascontiguousarray(np.broadcast_to(np.arange(512, dtype=f)[None, :], (128, 512)))
    return sh


def prep_core(inp, sh, core):
    f = np.float32
    b, half = divmod(core, 2)
    m = dict(sh)
    m["xT"] = np.ascontiguousarray(inp["x"][b, half * TOK:(half + 1) * TOK, :].T, dtype=f)
    m["cT"] = np.ascontiguousarray(inp["c"][b].reshape(8, 128).T, dtype=f)
    m["w_ada_h"] = np.ascontiguousarray(inp["w_ada"][:, :, half * 4608:(half + 1) * 4608], dtype=f)
    m["halom"] = np.full((128, 128), NEG if half == 0 else 0.0, f)
    m["flagb"] = np.full((128, 1), 0.0 if half == 0 else 1.0, f)
    return m


_NC = None


def kernel(**inputs):
    global _NC
    inp = {k: np.asarray(v) for k, v in inputs.items()}
    sh = prep_shared(inp)
    in_maps = [prep_core(inp, sh, c) for c in range(8)]
    if _NC is None:
        _NC = build()
    res = run_bass_kernel_spmd(_NC, in_maps, core_ids=list(range(8)))
    out = np.empty((4, 4096, D), np.float32)
    for c in range(8):
        b, half = divmod(c, 2)
        out[b, half * TOK:(half + 1) * TOK, :] = res.results[c]["outT"].T
    return out
```
